# Optimizing a Trainium2 kernel written in Bass

```python
import math
import jax, jax.numpy as jnp
from jax import lax
import numpy as np

D_MODEL = 2048
BATCH = 4
SEQ = 4096
DEPTH = 1

D_MIX = D_MODEL
GLA_WIDTH = D_MIX // 2
MLSTM_WIDTH = D_MIX - GLA_WIDTH
GLA_HEADS = 4
MLSTM_HEADS = 4
GLA_DV = GLA_WIDTH // GLA_HEADS
GLA_DK = GLA_DV // 2
MLSTM_DV = MLSTM_WIDTH // MLSTM_HEADS
MLSTM_DQK = MLSTM_DV // 2
GLA_QK = GLA_HEADS * GLA_DK
MLSTM_QK = MLSTM_HEADS * MLSTM_DQK
GLA_LOWRANK = 16
GLA_TAU = 16.0
CONV_WIDTH = 4
CHUNK = 64
D_FF = 4 * D_MODEL
ALPHA = (2.0 * DEPTH) ** 0.25
BETA = (8.0 * DEPTH) ** -0.25
LN_EPS = 1e-5
N_MOD = 6
SPLITS = (GLA_QK, GLA_QK, GLA_WIDTH, GLA_WIDTH, GLA_LOWRANK,
          MLSTM_QK, MLSTM_QK, MLSTM_WIDTH, MLSTM_WIDTH, MLSTM_HEADS, MLSTM_HEADS)
D_IN = sum(SPLITS)

kernel_name = "hybrid_gla_mlstm_deepnorm_adaln"


def _layer_norm(x, g, b):
    xf = x.astype(jnp.float32)
    mu = jnp.mean(xf, axis=-1, keepdims=True)
    var = jnp.mean(jnp.square(xf - mu), axis=-1, keepdims=True)
    return ((xf - mu) * lax.rsqrt(var + LN_EPS) * g + b).astype(x.dtype)


def _to_chunks(t, n_heads, d):
    b, s, _ = t.shape
    return t.reshape(b, s // CHUNK, CHUNK, n_heads, d).transpose(0, 3, 1, 2, 4)


def _gate_chunks(t):
    b, s, h = t.shape
    return t.reshape(b, s // CHUNK, CHUNK, h).transpose(0, 3, 1, 2)


def _from_chunks(t):
    b, h, n, c, d = t.shape
    return t.transpose(0, 2, 3, 1, 4).reshape(b, n * c, h, d)


def _causal_conv(x, w, b):
    s = x.shape[1]
    xp = jnp.pad(x, ((0, 0), (CONV_WIDTH - 1, 0), (0, 0)))
    out = b
    for j in range(CONV_WIDTH):
        out = out + xp[:, j:j + s] * w[j]
    return out


def _gla(q, k, v, r, a_lr, w_alpha_up, b_alpha, norm_g):
    bsz, s, _ = q.shape
    log_a = jax.nn.log_sigmoid(a_lr @ w_alpha_up + b_alpha) / GLA_TAU
    q = _to_chunks(q, GLA_HEADS, GLA_DK) * (GLA_DK ** -0.5)
    k = _to_chunks(k, GLA_HEADS, GLA_DK)
    v = _to_chunks(v, GLA_HEADS, GLA_DV)
    cum = jnp.cumsum(_to_chunks(log_a, GLA_HEADS, GLA_DK), axis=3)
    cum_last = cum[:, :, :, -1:, :]
    q_e = q * jnp.exp(cum)
    k_e = k * jnp.exp(-cum)
    causal = jnp.tril(jnp.ones((CHUNK, CHUNK), dtype=bool))
    scores = jnp.where(causal, jnp.einsum('bhncd,bhnsd->bhncs', q_e, k_e), 0.0)
    o_intra = jnp.einsum('bhncs,bhnsv->bhncv', scores, v)
    k_end = k * jnp.exp(cum_last - cum)
    d_state = jnp.einsum('bhncd,bhncv->bhndv', k_end, v)
    decay = jnp.exp(cum_last[:, :, :, 0, :])

    def step(state, inp):
        dec, ds = inp
        return dec[..., None] * state + ds, state

    s0 = jnp.zeros((bsz, GLA_HEADS, GLA_DK, GLA_DV), jnp.float32)
    _, s_prev = lax.scan(step, s0, (jnp.moveaxis(decay, 2, 0), jnp.moveaxis(d_state, 2, 0)))
    s_prev = jnp.moveaxis(s_prev, 0, 2)
    o_inter = jnp.einsum('bhncd,bhndv->bhncv', q_e, s_prev)
    o = _from_chunks(o_intra + o_inter)
    o = o * lax.rsqrt(jnp.mean(jnp.square(o), axis=-1, keepdims=True) + LN_EPS)
    o = o * norm_g.reshape(GLA_HEADS, GLA_DV)
    return o.reshape(bsz, s, GLA_WIDTH) * jax.nn.silu(r)


def _mlstm(q, k, v, o_pre, i_pre, f_pre, conv_w, conv_b, b_i, b_f, norm_g):
    bsz, s, _ = q.shape
    qk = jax.nn.silu(_causal_conv(jnp.concatenate([q, k], axis=-1), conv_w, conv_b))
    q, k = qk[..., :MLSTM_QK], qk[..., MLSTM_QK:]
    q = _to_chunks(q, MLSTM_HEADS, MLSTM_DQK) * (MLSTM_DQK ** -0.5)
    k = _to_chunks(k, MLSTM_HEADS, MLSTM_DQK)
    v = _to_chunks(v, MLSTM_HEADS, MLSTM_DV)
    i_log = _gate_chunks(i_pre + b_i)
    g = jnp.cumsum(_gate_chunks(jax.nn.log_sigmoid(f_pre + b_f)), axis=-1)
    g_last = g[..., -1]
    causal = jnp.tril(jnp.ones((CHUNK, CHUNK), dtype=bool))
    d_log = jnp.where(causal, g[..., :, None] - g[..., None, :] + i_log[..., None, :], -jnp.inf)
    m_intra = jnp.max(d_log, axis=-1)
    e_log = g_last[..., None] - g + i_log
    m_loc = jnp.max(e_log, axis=-1)
    w_end = jnp.exp(e_log - m_loc[..., None])
    d_c = jnp.einsum('bhnc,bhncd,bhncv->bhndv', w_end, k, v)
    d_n = jnp.einsum('bhnc,bhncd->bhnd', w_end, k)

    def step(carry, inp):
        c_st, n_st, m_st = carry
        gl, ml, dc, dn = inp
        m_new = jnp.maximum(gl + m_st, ml)
        a = jnp.exp(gl + m_st - m_new)
        bcoef = jnp.exp(ml - m_new)
        c_new = a[..., None, None] * c_st + bcoef[..., None, None] * dc
        n_new = a[..., None] * n_st + bcoef[..., None] * dn
        return (c_new, n_new, m_new), (c_st, n_st, m_st)

    init = (jnp.zeros((bsz, MLSTM_HEADS, MLSTM_DQK, MLSTM_DV), jnp.float32),
            jnp.zeros((bsz, MLSTM_HEADS, MLSTM_DQK), jnp.float32),
            jnp.zeros((bsz, MLSTM_HEADS), jnp.float32))
    xs = (jnp.moveaxis(g_last, 2, 0), jnp.moveaxis(m_loc, 2, 0),
          jnp.moveaxis(d_c, 2, 0), jnp.moveaxis(d_n, 2, 0))
    _, (c_prev, n_prev, m_prev) = lax.scan(step, init, xs)
    c_prev = jnp.moveaxis(c_prev, 0, 2)
    n_prev = jnp.moveaxis(n_prev, 0, 2)
    m_prev = jnp.moveaxis(m_prev, 0, 2)
    inter_log = g + m_prev[..., None]
    m_t = jnp.maximum(inter_log, m_intra)
    w_intra = jnp.exp(d_log - m_t[..., None])
    w_inter = jnp.exp(inter_log - m_t)
    sc = jnp.einsum('bhncd,bhnsd->bhncs', q, k) * w_intra
    num = jnp.einsum('bhncs,bhnsv->bhncv', sc, v) + \
        w_inter[..., None] * jnp.einsum('bhncd,bhndv->bhncv', q, c_prev)
    den = jnp.sum(sc, axis=-1) + w_inter * jnp.einsum('bhncd,bhnd->bhnc', q, n_prev)
    h = num / jnp.maximum(jnp.abs(den), jnp.exp(-m_t))[..., None]
    h = _from_chunks(h)
    mu = jnp.mean(h, axis=-1, keepdims=True)
    var = jnp.mean(jnp.square(h - mu), axis=-1, keepdims=True)
    h = (h - mu) * lax.rsqrt(var + LN_EPS) * norm_g.reshape(MLSTM_HEADS, MLSTM_DV)
    return jax.nn.sigmoid(o_pre) * h.reshape(bsz, s, MLSTM_WIDTH)


def _mixer(u, w_in, gla_w_alpha_up, gla_b_alpha, gla_norm_g, mlstm_conv_w, mlstm_conv_b,
           mlstm_b_i, mlstm_b_f, mlstm_norm_g, w_out):
    proj = (u @ w_in).astype(jnp.float32)
    idx = [int(v) for v in np.cumsum(SPLITS)[:-1]]
    gq, gk, gv, gr, ga, mq, mk, mv, mo, mi, mf = jnp.split(proj, idx, axis=-1)
    f32 = jnp.float32
    y_gla = _gla(gq, gk, gv, gr, ga, gla_w_alpha_up.astype(f32), gla_b_alpha.astype(f32),
                 gla_norm_g.astype(f32))
    y_ml = _mlstm(mq, mk, mv, mo, mi, mf, mlstm_conv_w.astype(f32), mlstm_conv_b.astype(f32),
                  mlstm_b_i.astype(f32), mlstm_b_f.astype(f32), mlstm_norm_g.astype(f32))
    y = jnp.concatenate([y_gla, y_ml], axis=-1).astype(u.dtype)
    return y @ w_out


def _sq_relu_mlp(u, w_ff1, w_ff2):
    return jnp.square(jax.nn.relu(u @ w_ff1)) @ w_ff2


def setup_inputs(seed: int = 0) -> dict:
    key = jax.random.key(seed)
    ks = jax.random.split(key, 20)
    nrm = jax.random.normal
    L, D = DEPTH, D_MODEL
    f_bias = jnp.linspace(3.0, 6.0, MLSTM_HEADS, dtype=jnp.float32)
    return {
        "x": nrm(ks[0], (BATCH, SEQ, D), jnp.float32),
        "c": nrm(ks[1], (BATCH, D), jnp.float32),
        "w_ada": nrm(ks[2], (L, D, N_MOD * D), jnp.float32) * (0.5 * D ** -0.5),
        "b_ada": nrm(ks[3], (L, N_MOD * D), jnp.float32) * 0.01,
        "w_in": nrm(ks[4], (L, D, D_IN), jnp.float32) * D ** -0.5,
        "gla_w_alpha_up": nrm(ks[5], (L, GLA_LOWRANK, GLA_QK), jnp.float32) * GLA_LOWRANK ** -0.5,
        "gla_b_alpha": nrm(ks[6], (L, GLA_QK), jnp.float32) * 0.1,
        "gla_norm_g": 1.0 + 0.02 * nrm(ks[7], (L, GLA_WIDTH), jnp.float32),
        "mlstm_conv_w": nrm(ks[8], (L, CONV_WIDTH, 2 * MLSTM_QK), jnp.float32) * CONV_WIDTH ** -0.5,
        "mlstm_conv_b": nrm(ks[9], (L, 2 * MLSTM_QK), jnp.float32) * 0.01,
        "mlstm_b_i": nrm(ks[10], (L, MLSTM_HEADS), jnp.float32) * 0.1,
        "mlstm_b_f": f_bias + 0.1 * nrm(ks[11], (L, MLSTM_HEADS), jnp.float32),
        "mlstm_norm_g": 1.0 + 0.02 * nrm(ks[12], (L, MLSTM_WIDTH), jnp.float32),
        "w_out": nrm(ks[13], (L, D_MIX, D), jnp.float32) * (BETA * D_MIX ** -0.5),
        "ln1_g": 1.0 + 0.02 * nrm(ks[14], (L, D), jnp.float32),
        "ln1_b": 0.02 * nrm(ks[15], (L, D), jnp.float32),
        "w_ff1": nrm(ks[16], (L, D, D_FF), jnp.float32) * D ** -0.5,
        "w_ff2": nrm(ks[17], (L, D_FF, D), jnp.float32) * (BETA * D_FF ** -0.5),
        "ln2_g": 1.0 + 0.02 * nrm(ks[18], (L, D), jnp.float32),
        "ln2_b": 0.02 * nrm(ks[19], (L, D), jnp.float32),
    }


def reference(x, c, w_ada, b_ada, w_in, gla_w_alpha_up, gla_b_alpha, gla_norm_g,
              mlstm_conv_w, mlstm_conv_b, mlstm_b_i, mlstm_b_f, mlstm_norm_g, w_out,
              ln1_g, ln1_b, w_ff1, w_ff2, ln2_g, ln2_b):
    cond = jax.nn.silu(c.astype(jnp.float32))
    for l in range(DEPTH):
        mod = (cond @ w_ada[l].astype(jnp.float32) + b_ada[l]).astype(x.dtype)
        sh1, sc1, g1, sh2, sc2, g2 = jnp.split(mod[:, None, :], N_MOD, axis=-1)
        u = x * (1.0 + sc1) + sh1
        y = _mixer(u, w_in[l], gla_w_alpha_up[l], gla_b_alpha[l], gla_norm_g[l],
                   mlstm_conv_w[l], mlstm_conv_b[l], mlstm_b_i[l], mlstm_b_f[l],
                   mlstm_norm_g[l], w_out[l])
        x = _layer_norm(ALPHA * x + g1 * y, ln1_g[l], ln1_b[l])
        u = x * (1.0 + sc2) + sh2
        y = _sq_relu_mlp(u, w_ff1[l], w_ff2[l])
        x = _layer_norm(ALPHA * x + g2 * y, ln2_g[l], ln2_b[l])
    return x
```

```python
import numpy as np
import ml_dtypes
from contextlib import ExitStack
import concourse.bass as bass
import concourse.mybir as mybir
from concourse.bass_utils import run_bass_kernel_spmd

F32 = mybir.dt.float32
BF16 = mybir.dt.bfloat16
AF = mybir.ActivationFunctionType
ALU = mybir.AluOpType
AX = mybir.AxisListType

D = 2048
SEQ = 4096
NB = 4
HALF = 2048
TB = 512
NTB_ALL = 8
DFF = 8192
ALPHA = 2.0 ** 0.25
EPS = 1e-5
N_TM = 4096
N_FM = 17 * 128
DBG = False
STOP = 99
TRACE = False


class Prog:
    def __init__(self, nc, n_dma_sems=24):
        self.nc = nc
        self.eng = {"pe": nc.tensor, "act": nc.scalar, "dve": nc.vector, "pool": nc.gpsimd, "sp": nc.sync}
        self.sem = {e: nc.alloc_semaphore("s_" + e) for e in ("pe", "act", "dve", "pool")}
        self.cnt = {e: 0 for e in self.sem}
        self.seen = {e: {} for e in self.eng}
        self.lastw = {}
        self.readers = {}
        self.pend = {e: ([], []) for e in self.sem}
        self.dsem = [nc.alloc_semaphore("d%d" % i) for i in range(n_dma_sems)]
        self.dcnt = [0] * n_dma_sems
        self.drr = 0
        self.rec = None

    def record(self, lst):
        self.rec = lst

    def play(self, streams):
        self.rec = None
        idx = [0] * len(streams)
        live = True
        while live:
            live = False
            for i, st in enumerate(streams):
                if idx[i] < len(st):
                    kind, a, k = st[idx[i]]
                    idx[i] += 1
                    live = True
                    (self.op if kind == "op" else self.dma)(*a, **k)

    def _sem(self, key):
        return self.dsem[key[1]] if isinstance(key, tuple) else self.sem[key]

    def _wait(self, e, dep):
        key, val = dep
        if val <= 0 or self.seen[e].get(key, 0) >= val:
            return
        self.seen[e][key] = val
        self.eng[e].wait_ge(self._sem(key), val)

    def _deps(self, e, reads, writes):
        deps = []
        for k in reads:
            if k in self.lastw:
                deps.append(self.lastw[k])
        for k in writes:
            if k in self.lastw:
                deps.append(self.lastw[k])
            deps += self.readers.get(k, [])
        for d in deps:
            if d[0] == "pe" and e == "pe":
                continue
            self._wait(e, d)

    def _mark(self, me, reads, writes):
        for k in reads:
            self.readers.setdefault(k, []).append(me)
        for k in writes:
            self.lastw[k] = me
            self.readers[k] = []

    def op(self, e, fn, reads=(), writes=(), sig=True):
        if self.rec is not None:
            self.rec.append(("op", (e, fn), dict(reads=list(reads), writes=list(writes), sig=sig)))
            return
        writes = list(writes) + [k for k in reads if k.startswith("ps")]
        reads = [k for k in reads if not k.startswith("ps")]
        self._deps(e, reads, writes)
        ins = fn(self.eng[e])
        if not sig:
            self.pend[e][0].extend(reads)
            self.pend[e][1].extend(writes)
            return
        self.cnt[e] += 1
        ins.then_inc(self.sem[e], 1)
        pr, pw = self.pend[e]
        self._mark((e, self.cnt[e]), reads + pr, writes + pw)
        self.pend[e] = ([], [])

    def dma(self, q, out, in_, reads=(), writes=()):
        if self.rec is not None:
            self.rec.append(("dma", (q, out, in_), dict(reads=list(reads), writes=list(writes))))
            return
        self._deps(q, reads, writes)
        i = self.drr
        self.drr = (self.drr + 1) % len(self.dsem)
        self._wait(q, (("d", i), self.dcnt[i]))
        self.dcnt[i] += 16
        self.eng[q].dma_start(out=out, in_=in_).then_inc(self.dsem[i], 16)
        self._mark((("d", i), self.dcnt[i]), list(reads), list(writes))

    def barrier(self):
        for e in self.eng:
            for f in self.sem:
                if f != e:
                    self._wait(e, (f, self.cnt[f]))
            for i in range(len(self.dsem)):
                self._wait(e, (("d", i), self.dcnt[i]))
        self.lastw = {}
        self.readers = {}


def build_nc():
    nc = bass.Bass("TRN2", target_bir_lowering=False)
    P = Prog(nc)

    def din(name, shape, dt=F32):
        return nc.dram_tensor(name, list(shape), dt, kind="ExternalInput").ap()

    xT = din("xT", [D, SEQ])
    flag_d = din("flag", [128, 1])
    cT_d = din("cT", [128, 16])
    wada_d = din("w_ada", [D, 6 * D])
    bada_d = din("b_ada", [128, 96])
    wtm_d = din("w_tm", [D, N_TM])
    wfm_d = din("w_fm", [D, N_FM])
    wup_d = din("w_up", [32, 512])
    ggla_d = din("g_gla", [128, 1024])
    gml_d = din("g_ml", [128, 1024])
    convw_d = din("convw", [128, 32])
    convb_d = din("convb", [128, 8])
    bi_d = din("b_i", [4, 1])
    bf_d = din("b_f", [4, 1])
    wout_d = din("w_out", [D, D])
    wff1_d = din("w_ff1", [D, DFF])
    wff2_d = din("w_ff2", [DFF, D])
    ln_d = din("ln", [128, 64])
    identb_d = din("ident_bf", [128, 128], BF16)
    identf_d = din("ident_f", [128, 128])
    ucum_d = din("ucum", [128, 128])
    urev_d = din("urev", [128, 128])
    maskT_d = din("maskT", [128, 128])
    rmask_d = din("rmask", [4, 512])
    sel_d = din("sel", [4, 512])
    onesb_d = din("ones_bf", [128, 128], BF16)
    outT = nc.dram_tensor("outT", [D, HALF], F32, kind="ExternalOutput").ap()

    skind = "ExternalOutput" if DBG else "Internal"
    P_tm = nc.dram_tensor("P_tm", [SEQ, N_TM], BF16, kind=skind).ap()
    P_fm = nc.dram_tensor("P_fm", [N_FM, SEQ], F32, kind=skind).ap()
    yT_d = nc.dram_tensor("yT_d", [D, HALF], BF16, kind=skind).ap()
    mod_d = nc.dram_tensor("mod_d", [1, 6 * D], F32, kind=skind).ap()
    wff1_b = nc.dram_tensor("wff1_b", [D, DFF], BF16, kind="Internal").ap()
    wff2_b = nc.dram_tensor("wff2_b", [DFF, D], BF16, kind="Internal").ap()

    with ExitStack() as G:
        def sb(name, shape, dt=F32, st=G):
            return st.enter_context(nc.sbuf_tensor("sb_" + name, list(shape), dt))

        ps = [G.enter_context(nc.psum_tensor("ps%d" % i, [128, 512], F32)) for i in range(7)]
        psb = G.enter_context(nc.psum_tensor("psb", [128, 1024], BF16))

        identb = sb("identb", [128, 128], BF16)
        identf = sb("identf", [128, 128])
        ucum = sb("ucum", [128, 128])
        urev = sb("urev", [128, 128])
        maskT = sb("maskT", [128, 128])
        rmask = sb("rmask", [4, 512])
        sel = sb("sel", [4, 512])
        onesb = sb("onesb", [128, 128], BF16)
        flag = sb("flag", [128, 1])
        cT = sb("cT", [128, 16])
        condb = sb("condb", [128, 16], BF16)
        bada = sb("bada", [128, 96])
        mod = sb("mod", [128, 96])
        sc1p = sb("sc1p", [128, 16])
        sc2p = sb("sc2p", [128, 16])
        lnp = sb("lnp", [128, 64])
        wup = sb("wup", [32, 512])
        convw = sb("convw", [128, 32])
        convb = sb("convb", [128, 8])
        b_i = sb("b_i", [4, 1])
        b_f = sb("b_f", [4, 1])
        nb_f = sb("nb_f", [4, 1])
        for t, d_, nm in ((identb, identb_d, "identb"), (identf, identf_d, "identf"), (ucum, ucum_d, "ucum"),
                          (urev, urev_d, "urev"), (maskT, maskT_d, "maskT"), (rmask, rmask_d, "rmask"),
                          (sel, sel_d, "sel"), (onesb, onesb_d, "onesb"), (flag, flag_d, "flag"), (cT, cT_d, "cT"),
                          (bada, bada_d, "bada"), (lnp, ln_d, "lnp"), (wup, wup_d, "wup"),
                          (convw, convw_d, "convw"), (convb, convb_d, "convb"),
                          (b_i, bi_d, "b_i"), (b_f, bf_d, "b_f")):
            P.dma("sp", t[:], d_, writes=[nm])
        P.op("dve", lambda e: e.tensor_scalar(out=nb_f[:], in0=b_f[:], scalar1=-1.0, scalar2=None, op0=ALU.mult),
             reads=["b_f"], writes=["nb_f"])

        def mod_groups(g0, g1, st):
            row = sb("modrow%d" % g0, [1, (g1 - g0) * 512], F32, st)
            wb = [sb("wada%d_%d" % (g0, i), [128, 16, 512], BF16, st) for i in range(2)]
            for g in range(g0, g1):
                w = wb[g % 2]
                wk = "wada%d" % (g % 2)
                P.dma("pool", w[:], wada_d[:, g * 512:(g + 1) * 512].rearrange("(k p) c -> p k c", p=128), writes=[wk])
                for k in range(16):
                    P.op("pe", lambda e, k=k, w=w: e.matmul(ps[0][0:1, :], lhsT=condb[:, k:k + 1], rhs=w[:, k, :],
                                                         start=(k == 0), stop=(k == 15)),
                         reads=[wk, "condb"], writes=["ps0"], sig=(k == 15))
                P.op("dve", lambda e, g=g: e.tensor_copy(out=row[0:1, (g - g0) * 512:(g - g0 + 1) * 512], in_=ps[0][0:1, :]),
                     reads=["ps0"], writes=["modrow"])
            P.dma("sp", mod_d[0:1, g0 * 512:g1 * 512], row[0:1, :], reads=["modrow"], writes=["mod_d"])
            nw = (g1 - g0) // 4
            w0 = g0 // 4
            P.dma("sp", mod[:, w0 * 16:(w0 + nw) * 16].rearrange("p (w k) -> p w k", k=16),
                  mod_d[0:1, g0 * 512:g1 * 512].rearrange("o (w p k) -> p (o w) k", p=128, k=16),
                  reads=["mod_d"], writes=["mod"])
            P.op("dve", lambda e: e.tensor_tensor(out=mod[:, w0 * 16:(w0 + nw) * 16], in0=mod[:, w0 * 16:(w0 + nw) * 16],
                                                  in1=bada[:, w0 * 16:(w0 + nw) * 16], op=ALU.add),
                 reads=["mod", "bada"], writes=["mod"])

        P.op("act", lambda e: e.activation(out=condb[:], in_=cT[:], func=AF.Silu), reads=["cT"], writes=["condb"])
        with ExitStack() as S0:
            mod_groups(0, 8, S0)
            P.op("dve", lambda e: e.tensor_scalar(out=sc1p[:], in0=mod[:, 16:32], scalar1=1.0, scalar2=None, op0=ALU.add),
                 reads=["mod"], writes=["sc1p"])
            P.barrier()

        with ExitStack() as S1:
            uT = sb("uT", [128, 16, SEQ], BF16, S1)
            xs = [sb("xs%d" % i, [128, 2, TB], F32, S1) for i in range(2)]
            wtm = [sb("wtm%d" % i, [128, 16, 512], BF16, S1) for i in range(2)]
            wfm = [sb("wfm%d" % i, [128, 16, 128], BF16, S1) for i in range(2)]
            stg_tm = [sb("stgtm%d" % i, [128, 4, 512], BF16, S1) for i in range(2)]
            stg_fm = [sb("stgfm%d" % i, [128, 512], F32, S1) for i in range(2)]
            nxs = [0]

            def modulate(tb):
                for kq in range(8):
                    x_ = xs[nxs[0] % 2]
                    xk = "xs%d" % (nxs[0] % 2)
                    nxs[0] += 1
                    P.dma("sp", x_[:], xT[kq * 256:(kq + 1) * 256, tb * TB:(tb + 1) * TB].rearrange("(k p) t -> p k t", p=128),
                          writes=[xk])
                    for kk in range(2):
                        k = kq * 2 + kk
                        P.op("act", lambda e, k=k, kk=kk, x_=x_, tb=tb: e.activation(
                            out=uT[:, k, tb * TB:(tb + 1) * TB], in_=x_[:, kk, :], func=AF.Identity,
                            scale=sc1p[:, k:k + 1], bias=mod[:, k:k + 1]),
                            reads=[xk, "sc1p", "mod"], writes=["uT%d" % tb])

            pending_mod = [4, 5, 6, 7, 0, 1, 2, 3]
            modulate(pending_mod.pop(0))
            ev = 0
            pbank = 0
            wad = [sb("wad%d" % i, [128, 16, 128], BF16, S1) for i in range(2)]
            mrow = [sb("mrow%d" % i, [1, 128], F32, S1) for i in range(2)]
            mstate = {"g": 32}

            def mod_step():
                g = mstate["g"]
                if g >= 96:
                    return
                mstate["g"] = g + 1
                w = wad[g % 2]
                wk = "wad%d" % (g % 2)
                r = mrow[g % 2]
                rk = "mrow%d" % (g % 2)
                P.dma("pool", w[:], wada_d[:, g * 128:(g + 1) * 128].rearrange("(k p) c -> p k c", p=128), writes=[wk])
                for k in range(16):
                    P.op("pe", lambda e, k=k, w=w: e.matmul(ps[4][0:1, 0:128], lhsT=condb[:, k:k + 1], rhs=w[:, k, :],
                                                         start=(k == 0), stop=(k == 15)),
                         reads=[wk, "condb"], writes=["ps4"], sig=(k == 15))
                P.op("dve", lambda e, r=r: e.tensor_copy(out=r[0:1, :], in_=ps[4][0:1, 0:128]), reads=["ps4"], writes=[rk])
                P.dma("sp", mod_d[0:1, g * 128:(g + 1) * 128], r[0:1, :], reads=[rk])
            for gi, g in enumerate((2, 3, 6, 7, 0, 1, 4, 5)):
                w = wtm[gi % 2]
                wk = "wtm%d" % (gi % 2)
                P.dma("pool", w[:], wtm_d[:, g * 512:(g + 1) * 512].rearrange("(k p) c -> p k c", p=128), writes=[wk])
                need_prefix = g in (0, 1, 4, 5)
                for tb in (4, 5, 6, 7, 0, 1, 2, 3):
                    if tb < 4 and not need_prefix:
                        continue
                    if pending_mod:
                        modulate(pending_mod.pop(0))
                    sg = stg_tm[ev % 2]
                    sk = "stgtm%d" % (ev % 2)
                    ev += 1
                    for tt in range(4):
                        pb = ps[pbank % 4]
                        pk = "ps%d" % (pbank % 4)
                        pbank += 1
                        t0 = tb * TB + tt * 128
                        for k in range(16):
                            P.op("pe", lambda e, k=k, pb=pb, t0=t0, w=w: e.matmul(
                                pb[:, :], lhsT=uT[:, k, t0:t0 + 128], rhs=w[:, k, :], start=(k == 0), stop=(k == 15)),
                                reads=[wk, "uT%d" % tb], writes=[pk], sig=(k == 15))
                        if tt % 2 == 0:
                            P.op("act", lambda e, pb=pb, sg=sg, tt=tt: e.activation(out=sg[:, tt, :], in_=pb[:, :], func=AF.Copy),
                                 reads=[pk], writes=[sk])
                        else:
                            P.op("dve", lambda e, pb=pb, sg=sg, tt=tt: e.tensor_copy(out=sg[:, tt, :], in_=pb[:, :]),
                                 reads=[pk], writes=[sk])
                    P.dma("sp", P_tm[tb * TB:(tb + 1) * TB, g * 512:(g + 1) * 512].rearrange("(t p) c -> p t c", p=128),
                          sg[:], reads=[sk])
                    mod_step()
            for g in range(17):
                w = wfm[g % 2]
                wk = "wfm%d" % (g % 2)
                P.dma("pool", w[:], wfm_d[:, g * 128:(g + 1) * 128].rearrange("(k p) c -> p k c", p=128), writes=[wk])
                need_prefix = (4 <= g < 8) or g >= 12
                for tb in range(NTB_ALL):
                    if tb < 4 and not need_prefix and not (8 <= g < 12 and tb == 3):
                        continue
                    sg = stg_fm[ev % 2]
                    sk = "stgfm%d" % (ev % 2)
                    ev += 1
                    pb = ps[pbank % 4]
                    pk = "ps%d" % (pbank % 4)
                    pbank += 1
                    for k in range(16):
                        P.op("pe", lambda e, k=k, pb=pb, tb=tb, w=w: e.matmul(
                            pb[:, :], lhsT=w[:, k, :], rhs=uT[:, k, tb * TB:(tb + 1) * TB], start=(k == 0), stop=(k == 15)),
                            reads=[wk, "uT%d" % tb], writes=[pk], sig=(k == 15))
                    if ev % 2 == 0:
                        P.op("act", lambda e, pb=pb, sg=sg: e.activation(out=sg[:], in_=pb[:, :], func=AF.Copy),
                             reads=[pk], writes=[sk])
                    else:
                        P.op("dve", lambda e, pb=pb, sg=sg: e.tensor_copy(out=sg[:], in_=pb[:, :]), reads=[pk], writes=[sk])
                    P.dma("sp", P_fm[g * 128:(g + 1) * 128, tb * TB:(tb + 1) * TB], sg[:], reads=[sk])
                    mod_step()
            while mstate["g"] < 96:
                mod_step()
            P.barrier()

        if STOP <= 1:
            return nc
        P.dma("sp", mod[:, 32:96].rearrange("p (w k) -> p w k", k=16),
              mod_d[0:1, 4096:12288].rearrange("o (w p k) -> p (o w) k", p=128, k=16), writes=["mod"])
        P.op("dve", lambda e: e.tensor_tensor(out=mod[:, 32:96], in0=mod[:, 32:96], in1=bada[:, 32:96], op=ALU.add),
             reads=["mod", "bada"], writes=["mod"])
        P.op("dve", lambda e: e.tensor_scalar(out=sc2p[:], in0=mod[:, 64:80], scalar1=1.0, scalar2=None, op0=ALU.add),
             reads=["mod"], writes=["sc2p"])
        P.barrier()
        if STOP <= 1.5:
            return nc
        with ExitStack() as S2:
            build_mixer(nc, P, S2, sb, ps, psb, dict(
                P_tm=P_tm, P_fm=P_fm, yT_d=yT_d, identb=identb, identf=identf, ucum=ucum, urev=urev, maskT=maskT,
                rmask=rmask, sel=sel, flag=flag, wup=wup, ggla_d=ggla_d, gml_d=gml_d, convw=convw, convb=convb,
                b_i=b_i, nb_f=nb_f, wff1_d=wff1_d, wff2_d=wff2_d, wff1_b=wff1_b, wff2_b=wff2_b))
            P.barrier()

        if STOP <= 2:
            return nc
        with ExitStack() as S3:
            build_dense(nc, P, S3, sb, ps, dict(
                xT=xT, yT_d=yT_d, wout_d=wout_d, wff1_d=wff1_b, wff2_d=wff2_b, outT=outT, mod=mod, sc2p=sc2p,
                lnp=lnp, onesb=onesb))
            P.barrier()
    return nc


def build_mixer(nc, P, S2, sb, ps, psb, C):
    P_tm, P_fm, yT_d = C["P_tm"], C["P_fm"], C["yT_d"]
    identb, identf, ucum, urev, maskT = C["identb"], C["identf"], C["ucum"], C["urev"], C["maskT"]
    rmask, sel, flag, wup = C["rmask"], C["sel"], C["flag"], C["wup"]
    convw, convb, b_i, nb_f = C["convw"], C["convb"], C["b_i"], C["nb_f"]
    GK = 128 ** -0.5

    def T(name, shape, dt=F32):
        return sb(name, shape, dt, S2)

    aT = T("aT", [32, TB])
    iT = T("iT", [4, TB])
    fT = T("fT", [4, TB])
    gq = [T("gq%d" % h, [128, TB]) for h in range(4)]
    gk = [T("gk%d" % h, [128, TB]) for h in range(4)]
    mqp = [T("mqp%d" % h, [128, TB + 3]) for h in range(4)]
    mkp = [T("mkp%d" % h, [128, TB + 3]) for h in range(4)]
    gv = T("gv", [128, 4, 1024], BF16)
    gr = T("gr", [128, 1024], BF16)
    vaug = T("vaug", [128, 4, 4, 257], BF16)
    mo = T("mo", [128, 1024], BF16)
    e1 = T("e1", [128, 512])
    sp = T("sp", [128, 4, 512])
    ecums = [T("ecum%d" % i, [128, TB]) for i in range(2)]
    encum = T("encum", [128, TB])
    erevs = [T("erev%d" % i, [128, TB]) for i in range(2)]
    qe = [T("qe%d" % h, [128, TB], BF16) for h in range(4)]
    ke = [T("ke%d" % h, [128, TB], BF16) for h in range(4)]
    kend = [T("kend%d" % h, [128, TB], BF16) for h in range(4)]
    dec = T("dec", [128, 4, 8])
    kend_tm = T("kend_tm", [128, 4, 4, 128], BF16)
    mk_tm = T("mk_tm", [128, 4, 4, 128], BF16)
    cacc = T("cacc", [128, TB])
    qc = [T("qc%d" % h, [128, TB], BF16) for h in range(4)]
    kc = [T("kc%d" % h, [128, TB], BF16) for h in range(4)]
    spf = T("spf", [4, TB])
    gpl = T("gpl", [4, TB])
    gpl2 = T("gpl2", [4, TB])
    am = T("am", [4, TB])
    amax = T("amax", [4, 8])
    Mn = T("Mn", [4, 8])
    mprev_all = T("mprev_all", [4, 9])
    dlog = T("dlog", [4, 8])
    wklog = am
    thrlog = gpl
    wkthr = T("wkthr", [128, 4, 8])
    wIb = T("wIb", [128, 32])
    vp = vaug
    scT = [T("scT%d" % i, [128, 128], BF16) for i in range(4)]
    Sg = [T("Sg%d" % h, [128, 256]) for h in range(4)]
    Sgb = [T("Sgb%d" % h, [128, 256], BF16) for h in range(4)]
    Cm = [T("Cm%d" % h, [128, 257]) for h in range(4)]
    Csb = [T("Csb%d" % h, [128, 257], BF16) for h in range(4)]
    silr = T("silr", [128, 1024], BF16)
    sigo = T("sigo", [128, 1024], BF16)
    wk4 = [T("wk4_%d" % h, [128, 256]) for h in range(4)]
    ss = T("ss", [128, 4])
    dn = T("dn", [128, 4])
    rs4 = T("rs4", [128, 4])
    bst = T("bst", [128, 4, 6])
    mv2 = T("mv2", [128, 4, 2])
    sm = T("sm", [128, 4])
    y_tm = T("y_tm", [128, 4, 2048], BF16)
    yTs = T("yTs", [128, 16, TB], BF16)

    ggla = T("ggla", [128, 1024])
    gml = T("gml", [128, 1024])
    P.dma("sp", ggla[:], C["ggla_d"], writes=["ggla"])
    P.dma("sp", gml[:], C["gml_d"], writes=["gml"])
    cstg = [T("cstg%d" % i, [128, 16, 256], BF16) for i in range(2)]
    ctasks = []
    for cg in range(32):
        ctasks.append((C["wff1_d"][:, cg * 256:(cg + 1) * 256].rearrange("(k p) c -> p k c", p=128),
                       C["wff1_b"][:, cg * 256:(cg + 1) * 256].rearrange("(k p) c -> p k c", p=128)))
    for fq in range(4):
        for cg in range(8):
            ctasks.append((C["wff2_d"][fq * 2048:(fq + 1) * 2048, cg * 256:(cg + 1) * 256].rearrange("(k p) c -> p k c", p=128),
                           C["wff2_b"][fq * 2048:(fq + 1) * 2048, cg * 256:(cg + 1) * 256].rearrange("(k p) c -> p k c", p=128)))
    cstate = {"next": 0, "pending": []}

    def conv_step():
        for (i, dst) in cstate["pending"]:
            P.dma("sp", dst, cstg[i][:], reads=["cstg%d" % i])
        cstate["pending"] = []
        for i in range(2):
            if cstate["next"] < len(ctasks):
                src, dst = ctasks[cstate["next"]]
                cstate["next"] += 1
                P.dma("pool", cstg[i][:], src, writes=["cstg%d" % i])
                cstate["pending"].append((i, dst))

    P.op("pool", lambda e: e.memset(aT[:], 1.0), writes=["aT"])
    P.op("pool", lambda e: e.memset(vaug[:].rearrange("p a b c -> p (a b c)"), 1.0), writes=["vaug"])
    P.op("pool", lambda e: e.memset(mprev_all[:], 0.0), writes=["mprev_all"])
    for h in range(4):
        P.op("pool", lambda e, h=h: e.memset(Sg[h][:], 0.0), writes=["Sg%d" % h])
        P.op("pool", lambda e, h=h: e.memset(Sgb[h][:], 0.0), writes=["Sgb%d" % h])
        P.op("pool", lambda e, h=h: e.memset(Cm[h][:], 0.0), writes=["Cm%d" % h])
        P.op("pool", lambda e, h=h: e.memset(mqp[h][:, 0:3], 0.0), writes=["mqp%d" % h])
        P.op("pool", lambda e, h=h: e.memset(mkp[h][:, 0:3], 0.0), writes=["mkp%d" % h])

    for blk in range(NTB_ALL):
        own = blk >= 4
        t0 = blk * TB
        tsl = slice(t0, t0 + TB)
        P.dma("sp", aT[0:16, :], P_fm[16 * 128:16 * 128 + 16, tsl], writes=["aT"])
        P.dma("sp", iT[:], P_fm[16 * 128 + 32:16 * 128 + 36, tsl], writes=["iT"])
        P.dma("sp", fT[:], P_fm[16 * 128 + 64:16 * 128 + 68, tsl], writes=["fT"])
        for h in range(4):
            P.dma("sp", gk[h][:], P_fm[(4 + h) * 128:(5 + h) * 128, tsl], writes=["gk%d" % h])
            if own:
                P.dma("sp", gq[h][:], P_fm[h * 128:(h + 1) * 128, tsl], writes=["gq%d" % h])
            if blk == 0:
                P.dma("sp", mkp[h][:, 3:], P_fm[(12 + h) * 128:(13 + h) * 128, tsl], writes=["mkp%d" % h])
            else:
                P.dma("sp", mkp[h][:], P_fm[(12 + h) * 128:(13 + h) * 128, t0 - 3:t0 + TB], writes=["mkp%d" % h])
                if own:
                    P.dma("sp", mqp[h][:], P_fm[(8 + h) * 128:(9 + h) * 128, t0 - 3:t0 + TB], writes=["mqp%d" % h])
            if blk == 4:
                P.op("dve", lambda e, h=h: e.tensor_scalar(out=mkp[h][:, 0:3], in0=mkp[h][:, 0:3], scalar1=flag[:, 0:1],
                                                           scalar2=None, op0=ALU.mult), reads=["mkp%d" % h, "flag"], writes=["mkp%d" % h])
                P.op("dve", lambda e, h=h: e.tensor_scalar(out=mqp[h][:, 0:3], in0=mqp[h][:, 0:3], scalar1=flag[:, 0:1],
                                                           scalar2=None, op0=ALU.mult), reads=["mqp%d" % h, "flag"], writes=["mqp%d" % h])
        rows = P_tm[tsl, :].rearrange("(t p) c -> p t c", p=128)
        P.dma("sp", gv[:], rows[:, :, 0:1024], writes=["gv"])
        for h in range(4):
            P.dma("sp", vaug[:, :, h, 0:256], rows[:, :, 2048 + h * 256:2048 + (h + 1) * 256], writes=["vaug"])
        P.op("pool", lambda e: e.memset(vaug[:, :, :, 256:257], 1.0), writes=["vaug"])
        if blk == 4:
            for h in range(4):
                P.op("dve", lambda e, h=h: e.tensor_scalar(out=Sg[h][:], in0=Sg[h][:], scalar1=flag[:, 0:1], scalar2=None, op0=ALU.mult),
                     reads=["Sg%d" % h, "flag"], writes=["Sg%d" % h])
                P.op("dve", lambda e, h=h: e.tensor_scalar(out=Sgb[h][:], in0=Sgb[h][:], scalar1=flag[:, 0:1], scalar2=None, op0=ALU.mult),
                     reads=["Sgb%d" % h, "flag"], writes=["Sgb%d" % h])
                P.op("dve", lambda e, h=h: e.tensor_scalar(out=Cm[h][:], in0=Cm[h][:], scalar1=flag[:, 0:1], scalar2=None, op0=ALU.mult),
                     reads=["Cm%d" % h, "flag"], writes=["Cm%d" % h])

        stG, stC, stM = [], [], []
        P.record(stG)
        for tt in range(4):
            P.op("pe", lambda e, tt=tt: e.matmul(ps[0][:, :], lhsT=aT[0:32, tt * 128:(tt + 1) * 128], rhs=wup[0:32, :], start=True, stop=True),
                 reads=["aT", "wup"], writes=["ps0"])
            P.op("act", lambda e: e.activation(out=e1[:], in_=ps[0][:, :], func=AF.Exp, scale=-1.0), reads=["ps0"], writes=["e1"])
            P.op("act", lambda e, tt=tt: e.activation(out=sp[:, tt, :], in_=e1[:], func=AF.Ln, bias=1.0), reads=["e1"], writes=["sp"])
        for h in range(4):
            ecum, erev = ecums[h % 2], erevs[h % 2]
            eck, erk = "ecum%d" % (h % 2), "erev%d" % (h % 2)
            for tt in range(4):
                P.op("pe", lambda e, h=h, tt=tt: e.matmul(ps[1][:, tt * 128:(tt + 1) * 128], lhsT=sp[:, tt, h * 128:(h + 1) * 128], rhs=ucum[:, :],
                                                          start=True, stop=True), reads=["sp", "ucum"], writes=["ps1"], sig=(tt == 3))
            for tt in range(4):
                P.op("pe", lambda e, h=h, tt=tt: e.matmul(ps[2][:, tt * 128:(tt + 1) * 128], lhsT=sp[:, tt, h * 128:(h + 1) * 128], rhs=urev[:, :],
                                                          start=True, stop=True), reads=["sp", "urev"], writes=["ps2"], sig=(tt == 3))
            P.op("act", lambda e, ecum=ecum: e.activation(out=ecum[:], in_=ps[1][:, :], func=AF.Exp), reads=["ps1"], writes=[eck])
            P.op("act", lambda e, erev=erev: e.activation(out=erev[:], in_=ps[2][:, :], func=AF.Exp), reads=["ps2"], writes=[erk])
            if own:
                P.op("act", lambda e: e.activation(out=encum[:], in_=ps[1][:, :], func=AF.Exp, scale=-1.0), reads=["ps1"], writes=["encum"])
                P.op("dve", lambda e, h=h, ecum=ecum: e.scalar_tensor_tensor(out=qe[h][:], in0=gq[h][:], scalar=GK, in1=ecum[:], op0=ALU.mult, op1=ALU.mult),
                     reads=["gq%d" % h, eck], writes=["qe%d" % h])
                P.op("dve", lambda e, h=h: e.tensor_tensor(out=ke[h][:], in0=gk[h][:], in1=encum[:], op=ALU.mult),
                     reads=["gk%d" % h, "encum"], writes=["ke%d" % h])
            P.op("dve", lambda e, h=h, erev=erev: e.tensor_tensor(out=kend[h][:], in0=gk[h][:], in1=erev[:], op=ALU.mult),
                 reads=["gk%d" % h, erk], writes=["kend%d" % h])
            P.op("dve", lambda e, h=h, ecum=ecum: e.tensor_copy(out=dec[:, h, :], in_=ecum[:].rearrange("p (n c) -> p n c", c=64)[:, :, 63]),
                 reads=[eck], writes=["dec%d" % h])
        for tt in range(4):
            for h in range(4):
                P.op("pe", lambda e, h=h, tt=tt: e.transpose(psb[:, h * 128:(h + 1) * 128], kend[h][:, tt * 128:(tt + 1) * 128], identb[:, :]),
                     reads=["kend%d" % h, "identb"], writes=["psb"], sig=(h == 3))
            P.op("act", lambda e, tt=tt: e.activation(out=kend_tm[:, tt].rearrange("p h d -> p (h d)"), in_=psb[:, 0:512], func=AF.Copy),
                 reads=["psb"], writes=["kend_tm"])

        P.record(stC)
        for h in range(4):
            for qk, (pre, dst) in enumerate(((mqp[h], qc[h]), (mkp[h], kc[h]))):
                if qk == 0 and not own:
                    continue
                pk = ("mqp%d" if qk == 0 else "mkp%d") % h
                dk_ = ("qc%d" if qk == 0 else "kc%d") % h
                ci = qk * 4 + h
                P.op("dve", lambda e, pre=pre, ci=ci: e.tensor_scalar(out=cacc[:], in0=pre[:, 3:TB + 3], scalar1=convw[:, ci * 4 + 3:ci * 4 + 4],
                                                                      scalar2=convb[:, ci:ci + 1], op0=ALU.mult, op1=ALU.add),
                     reads=[pk, "convw", "convb"], writes=["cacc"])
                for j in (2, 1, 0):
                    P.op("dve", lambda e, pre=pre, ci=ci, j=j: e.scalar_tensor_tensor(out=cacc[:], in0=pre[:, j:TB + j], scalar=convw[:, ci * 4 + j:ci * 4 + j + 1],
                                                                                   in1=cacc[:], op0=ALU.mult, op1=ALU.add),
                         reads=[pk, "convw", "cacc"], writes=["cacc"])
                if qk == 0:
                    P.op("act", lambda e: e.activation(out=cacc[:], in_=cacc[:], func=AF.Silu), reads=["cacc"], writes=["cacc"])
                    P.op("pool", lambda e, dst=dst: e.tensor_scalar(out=dst[:], in0=cacc[:], scalar1=GK, scalar2=0.0, op0=ALU.mult, op1=ALU.add),
                         reads=["cacc"], writes=[dk_])
                else:
                    P.op("act", lambda e, dst=dst: e.activation(out=dst[:], in_=cacc[:], func=AF.Silu), reads=["cacc"], writes=[dk_])
        P.record(stM)
        P.op("act", lambda e: e.activation(out=spf[:], in_=fT[:], func=AF.Exp, scale=-1.0, bias=nb_f[:, 0:1]), reads=["fT", "nb_f"], writes=["spf"])
        P.op("act", lambda e: e.activation(out=spf[:], in_=spf[:], func=AF.Ln, bias=1.0), reads=["spf"], writes=["spf"])
        P.op("pool", lambda e: e.tensor_copy(out=gpl[:], in_=spf[:]), reads=["spf"], writes=["gpl"])
        cur, nxt, ck, nk = gpl, gpl2, "gpl", "gpl2"
        for sh in (1, 2, 4, 8, 16, 32):
            cv = cur[:].rearrange("p (n c) -> p n c", c=64)
            nv = nxt[:].rearrange("p (n c) -> p n c", c=64)
            P.op("dve", lambda e, cv=cv, nv=nv, sh=sh: e.tensor_tensor(out=nv[:, :, sh:64], in0=cv[:, :, sh:64], in1=cv[:, :, 0:64 - sh], op=ALU.add),
                 reads=[ck], writes=[nk])
            P.op("dve", lambda e, cv=cv, nv=nv, sh=sh: e.tensor_copy(out=nv[:, :, 0:sh], in_=cv[:, :, 0:sh]), reads=[ck], writes=[nk])
            cur, nxt, ck, nk = nxt, cur, nk, ck
        P.op("dve", lambda e: e.scalar_tensor_tensor(out=am[:], in0=iT[:], scalar=b_i[:, 0:1], in1=gpl[:], op0=ALU.add, op1=ALU.add),
             reads=["iT", "b_i", "gpl"], writes=["am"])
        P.op("dve", lambda e: e.tensor_reduce(out=amax[:], in_=am[:].rearrange("p (n c) -> p n c", c=64), axis=AX.X, op=ALU.max),
             reads=["am"], writes=["amax"])
        if blk > 0:
            P.op("dve", lambda e: e.tensor_copy(out=mprev_all[:, 0:1], in_=mprev_all[:, 8:9]), reads=["mprev_all"], writes=["mprev_all"])
        if blk == 4:
            P.op("dve", lambda e: e.tensor_scalar(out=mprev_all[:, 0:1], in0=mprev_all[:, 0:1], scalar1=flag[0:4, 0:1], scalar2=None, op0=ALU.mult),
                 reads=["mprev_all", "flag"], writes=["mprev_all"])
        gl = gpl[:].rearrange("p (n c) -> p n c", c=64)
        for n in range(8):
            P.op("dve", lambda e, n=n: e.tensor_tensor(out=Mn[:, n:n + 1], in0=mprev_all[:, n:n + 1], in1=amax[:, n:n + 1], op=ALU.max),
                 reads=["mprev_all", "amax"], writes=["Mn"])
            P.op("dve", lambda e, n=n: e.tensor_tensor(out=mprev_all[:, n + 1:n + 2], in0=Mn[:, n:n + 1], in1=gl[:, n, 63:64], op=ALU.subtract),
                 reads=["Mn", "gpl"], writes=["mprev_all"])
        P.op("dve", lambda e: e.tensor_tensor(out=dlog[:], in0=mprev_all[:, 0:8], in1=Mn[:], op=ALU.subtract),
             reads=["mprev_all", "Mn"], writes=["dlog"])
        Mb = Mn[:].rearrange("p (n o) -> p n o", o=1).broadcast_to([4, 8, 64])
        P.op("dve", lambda e: e.tensor_tensor(out=wklog[:].rearrange("p (n c) -> p n c", c=64), in0=am[:].rearrange("p (n c) -> p n c", c=64),
                                              in1=Mb, op=ALU.subtract), reads=["am", "Mn"], writes=["am"])
        P.op("dve", lambda e: e.tensor_tensor(out=thrlog[:].rearrange("p (n c) -> p n c", c=64), in0=gpl[:].rearrange("p (n c) -> p n c", c=64),
                                              in1=Mb, op=ALU.subtract), reads=["gpl", "Mn"], writes=["gpl"])
        for tt in range(4):
            P.op("pe", lambda e, tt=tt: e.matmul(ps[5][:, tt * 8:tt * 8 + 4], lhsT=wklog[:, tt * 128:(tt + 1) * 128], rhs=identf[0:4, 0:4], start=True, stop=True),
                 reads=["am", "identf"], writes=["ps5"], sig=False)
            P.op("pe", lambda e, tt=tt: e.matmul(ps[5][:, tt * 8 + 4:tt * 8 + 8], lhsT=thrlog[:, tt * 128:(tt + 1) * 128], rhs=identf[0:4, 0:4], start=True, stop=True),
                 reads=["gpl", "identf"], writes=["ps5"], sig=(tt == 3))
        P.op("act", lambda e: e.activation(out=wkthr[:].rearrange("p t c -> p (t c)"), in_=ps[5][:, 0:32], func=AF.Exp), reads=["ps5"], writes=["wkthr"])
        for h in range(4):
            P.op("pe", lambda e, h=h: e.matmul(ps[5][:, 64 + h * 8:64 + h * 8 + 8], lhsT=sel[:, h * 128:(h + 1) * 128], rhs=dlog[:, :], start=True, stop=True),
                 reads=["sel", "dlog"], writes=["ps5"], sig=(h == 3))
        P.op("act", lambda e: e.activation(out=wIb[:], in_=ps[5][:, 64:96], func=AF.Exp), reads=["ps5"], writes=["wIb"])
        for tt in range(4):
            for h in range(4):
                P.op("pool", lambda e, tt=tt, h=h: e.tensor_scalar(out=vp[:, tt, h, :], in0=vaug[:, tt, h, :], scalar1=wkthr[:, tt, h:h + 1], scalar2=0.0,
                                                                  op0=ALU.mult, op1=ALU.add), reads=["vaug", "wkthr"], writes=["vaug"])
        P.record(stC)
        for tt in range(4):
            for h in range(4):
                P.op("pe", lambda e, h=h, tt=tt: e.transpose(psb[:, 512 + h * 128:512 + (h + 1) * 128], kc[h][:, tt * 128:(tt + 1) * 128], identb[:, :]),
                     reads=["kc%d" % h, "identb"], writes=["psb"], sig=(h == 3))
            P.op("act", lambda e, tt=tt: e.activation(out=mk_tm[:, tt].rearrange("p h d -> p (h d)"), in_=psb[:, 512:1024], func=AF.Copy),
                 reads=["psb"], writes=["mk_tm"])

        P.play([stG, stC, stM])
        OB = [ps[1], ps[2], ps[3], ps[4]]
        OBK = ["ps1", "ps2", "ps3", "ps4"]
        DP = [ps[6], ps[0]]
        DPK = ["ps6", "ps0"]
        ndp = 0
        for tt in range(4):
            c0 = tt * 128
            conv_step()
            if own:
                P.dma("sp", gr[:], P_tm[t0 + c0:t0 + c0 + 128, 1024:2048], writes=["gr"])
                P.dma("sp", mo[:], P_tm[t0 + c0:t0 + c0 + 128, 3072:4096], writes=["mo"])
                P.op("act", lambda e: e.activation(out=silr[:], in_=gr[:], func=AF.Silu), reads=["gr"], writes=["silr"])
                P.op("act", lambda e: e.activation(out=sigo[:], in_=mo[:], func=AF.Sigmoid), reads=["mo"], writes=["sigo"])
            if own:
                for h in range(4):
                    P.op("pe", lambda e, h=h, c0=c0: e.matmul(ps[5][:, 0:128], lhsT=ke[h][:, c0:c0 + 128], rhs=qe[h][:, c0:c0 + 128], start=True, stop=True),
                         reads=["ke%d" % h, "qe%d" % h], writes=["ps5"])
                    P.op("dve", lambda e, h=h: e.tensor_tensor(out=scT[h][:], in0=ps[5][:, 0:128], in1=maskT[:], op=ALU.mult),
                         reads=["ps5", "maskT"], writes=["scT%d" % h])
                    P.op("pe", lambda e, h=h, tt=tt: e.matmul(OB[h][:, 0:256], lhsT=scT[h][:], rhs=gv[:, tt, h * 256:(h + 1) * 256], start=True, stop=False),
                         reads=["scT%d" % h, "gv"], writes=[OBK[h]], sig=False)
            for half in range(2):
                n = tt * 2 + half
                r0 = half * 64
                for h in range(4):
                    if own:
                        P.op("pe", lambda e, h=h, c0=c0, r0=r0: e.matmul(OB[h][r0:r0 + 64, 0:256], lhsT=qe[h][:, c0 + r0:c0 + r0 + 64], rhs=Sgb[h][:],
                                                                      start=False, stop=(r0 == 64)),
                             reads=["qe%d" % h, "Sgb%d" % h], writes=[OBK[h]], sig=(half == 1))
                    dp, dpk = DP[ndp % 2], DPK[ndp % 2]
                    ndp += 1
                    P.op("pe", lambda e, h=h, tt=tt, r0=r0, dp=dp: e.matmul(dp[:, 0:256], lhsT=kend_tm[r0:r0 + 64, tt, h, :], rhs=gv[r0:r0 + 64, tt, h * 256:(h + 1) * 256],
                                                                          start=True, stop=True), reads=["kend_tm", "gv"], writes=[dpk])
                    P.op("dve", lambda e, h=h, n=n, dp=dp: e.scalar_tensor_tensor(out=Sg[h][:], in0=Sg[h][:], scalar=dec[:, h, n:n + 1], in1=dp[:, 0:256],
                                                                                op0=ALU.mult, op1=ALU.add), reads=["Sg%d" % h, "dec%d" % h, dpk], writes=["Sg%d" % h])
                    P.op("act", lambda e, h=h: e.activation(out=Sgb[h][:], in_=Sg[h][:], func=AF.Copy), reads=["Sg%d" % h], writes=["Sgb%d" % h])
            if own:
                for h in range(4):
                    P.op("act", lambda e, h=h: e.activation(out=wk4[h][:], in_=OB[h][:, 0:256], func=AF.Square), reads=[OBK[h]], writes=["wk4_%d" % h])
                for h in range(4):
                    P.op("dve", lambda e, h=h: e.tensor_reduce(out=ss[:, h:h + 1], in_=wk4[h][:], axis=AX.X, op=ALU.add),
                         reads=["wk4_%d" % h], writes=["ss%d" % h])
                P.op("dve", lambda e: e.tensor_scalar(out=sm[:, 0:4], in0=ss[:, 0:4], scalar1=1.0 / 256, scalar2=EPS, op0=ALU.mult, op1=ALU.add),
                     reads=["ss0", "ss1", "ss2", "ss3"], writes=["sm"])
                P.op("act", lambda e: e.activation(out=sm[:, 0:4], in_=sm[:, 0:4], func=AF.Sqrt), reads=["sm"], writes=["sm"])
                P.op("dve", lambda e: e.reciprocal(out=sm[:, 0:4], in_=sm[:, 0:4]), reads=["sm"], writes=["sm"])
                for h in range(4):
                    P.op("dve", lambda e, h=h: e.scalar_tensor_tensor(out=wk4[h][:], in0=OB[h][:, 0:256], scalar=sm[:, h:h + 1], in1=ggla[:, h * 256:(h + 1) * 256],
                                                                    op0=ALU.mult, op1=ALU.mult), reads=[OBK[h], "sm", "ggla"], writes=["wk4_%d" % h])
                for h in range(4):
                    P.op("pool", lambda e, tt=tt, h=h: e.tensor_tensor(out=y_tm[:, tt, h * 256:(h + 1) * 256], in0=wk4[h][:], in1=silr[:, h * 256:(h + 1) * 256], op=ALU.mult),
                         reads=["wk4_%d" % h, "silr"], writes=["y_tm"])
            if own:
                for h in range(4):
                    P.op("pe", lambda e, h=h, c0=c0: e.matmul(ps[5][:, 0:128], lhsT=kc[h][:, c0:c0 + 128], rhs=qc[h][:, c0:c0 + 128], start=True, stop=True),
                         reads=["kc%d" % h, "qc%d" % h], writes=["ps5"])
                    P.op("dve", lambda e, h=h: e.tensor_tensor(out=scT[h][:], in0=ps[5][:, 0:128], in1=maskT[:], op=ALU.mult),
                         reads=["ps5", "maskT"], writes=["scT%d" % h])
                    P.op("pe", lambda e, h=h, tt=tt: e.matmul(OB[h][:, 0:257], lhsT=scT[h][:], rhs=vp[:, tt, h, :], start=True, stop=False),
                         reads=["scT%d" % h, "vaug"], writes=[OBK[h]], sig=False)
            for half in range(2):
                n = tt * 2 + half
                r0 = half * 64
                for h in range(4):
                    if own:
                        P.op("pool", lambda e, h=h, n=n: e.tensor_scalar(out=Csb[h][:], in0=Cm[h][:], scalar1=wIb[:, h * 8 + n:h * 8 + n + 1], scalar2=0.0, op0=ALU.mult, op1=ALU.add),
                             reads=["Cm%d" % h, "wIb"], writes=["Csb%d" % h])
                        P.op("pe", lambda e, h=h, c0=c0, r0=r0: e.matmul(OB[h][r0:r0 + 64, 0:257], lhsT=qc[h][:, c0 + r0:c0 + r0 + 64], rhs=Csb[h][:],
                                                                      start=False, stop=(r0 == 64)),
                             reads=["qc%d" % h, "Csb%d" % h], writes=[OBK[h]], sig=(half == 1))
                    dp, dpk = DP[ndp % 2], DPK[ndp % 2]
                    ndp += 1
                    P.op("pe", lambda e, h=h, tt=tt, r0=r0, dp=dp: e.matmul(dp[:, 0:257], lhsT=mk_tm[r0:r0 + 64, tt, h, :], rhs=vp[r0:r0 + 64, tt, h, :],
                                                                          start=True, stop=True), reads=["mk_tm", "vaug"], writes=[dpk])
                    P.op("dve", lambda e, h=h, n=n, dp=dp: e.scalar_tensor_tensor(out=Cm[h][:], in0=Cm[h][:], scalar=wIb[:, h * 8 + n:h * 8 + n + 1], in1=dp[:, 0:257],
                                                                                op0=ALU.mult, op1=ALU.add), reads=["Cm%d" % h, "wIb", dpk], writes=["Cm%d" % h])
            if own:
                for h in range(4):
                    P.op("act", lambda e, h=h: e.activation(out=dn[:, h:h + 1], in_=OB[h][:, 256:257], func=AF.Copy), reads=[OBK[h]], writes=["dn%d" % h])
                P.op("dve", lambda e: e.tensor_scalar(out=ss[:, 0:4], in0=dn[:, 0:4], scalar1=-1.0, scalar2=None, op0=ALU.mult),
                     reads=["dn0", "dn1", "dn2", "dn3", "ss0", "ss1", "ss2", "ss3"], writes=["ssb"])
                P.op("dve", lambda e: e.tensor_tensor(out=ss[:, 0:4], in0=dn[:, 0:4], in1=ss[:, 0:4], op=ALU.max), reads=["ssb", "dn0", "dn1", "dn2", "dn3"], writes=["ssb"])
                P.op("dve", lambda e, tt=tt: e.tensor_tensor(out=sm[:, 0:4], in0=ss[:, 0:4], in1=wkthr[:, tt, 4:8], op=ALU.max), reads=["ssb", "wkthr"], writes=["sm"])
                P.op("dve", lambda e: e.reciprocal(out=sm[:, 0:4], in_=sm[:, 0:4]), reads=["sm"], writes=["sm"])
                for h in range(4):
                    P.op("dve", lambda e, h=h: e.tensor_scalar(out=wk4[h][:], in0=OB[h][:, 0:256], scalar1=sm[:, h:h + 1], scalar2=None, op0=ALU.mult),
                         reads=[OBK[h], "sm"], writes=["wk4_%d" % h])
                for h in range(4):
                    P.op("dve", lambda e, h=h: e.bn_stats(out=bst[:, h, :], in_=wk4[h][:]), reads=["wk4_%d" % h], writes=["bst%d" % h])
                for h in range(4):
                    P.op("dve", lambda e, h=h: e.bn_aggr(out=mv2[:, h, :], in_=bst[:, h, :]), reads=["bst%d" % h], writes=["mv%d" % h])
                P.op("dve", lambda e: e.tensor_scalar(out=rs4[:, 0:4], in0=mv2[:, :, 1], scalar1=EPS, scalar2=None, op0=ALU.add),
                     reads=["mv0", "mv1", "mv2", "mv3"], writes=["rs4"])
                P.op("act", lambda e: e.activation(out=rs4[:, 0:4], in_=rs4[:, 0:4], func=AF.Sqrt), reads=["rs4"], writes=["rs4"])
                P.op("dve", lambda e: e.reciprocal(out=rs4[:, 0:4], in_=rs4[:, 0:4]), reads=["rs4"], writes=["rs4"])
                for h in range(4):
                    P.op("dve", lambda e, h=h: e.tensor_scalar(out=wk4[h][:], in0=wk4[h][:], scalar1=mv2[:, h, 0:1], scalar2=rs4[:, h:h + 1], op0=ALU.subtract, op1=ALU.mult),
                         reads=["wk4_%d" % h, "mv%d" % h, "rs4"], writes=["wk4_%d" % h])
                for h in range(4):
                    P.op("pool", lambda e, h=h: e.tensor_tensor(out=wk4[h][:], in0=wk4[h][:], in1=gml[:, h * 256:(h + 1) * 256], op=ALU.mult),
                         reads=["wk4_%d" % h, "gml"], writes=["wk4_%d" % h])
                for h in range(4):
                    P.op("pool", lambda e, tt=tt, h=h: e.tensor_tensor(out=y_tm[:, tt, 1024 + h * 256:1024 + (h + 1) * 256], in0=wk4[h][:], in1=sigo[:, h * 256:(h + 1) * 256], op=ALU.mult),
                         reads=["wk4_%d" % h, "sigo"], writes=["y_tm"])
        if own:
            for k in range(16):
                for tt in range(4):
                    P.op("pe", lambda e, k=k, tt=tt: e.transpose(psb[:, tt * 128:(tt + 1) * 128], y_tm[:, tt, k * 128:(k + 1) * 128], identb[:, :]),
                         reads=["y_tm", "identb"], writes=["psb"], sig=(tt == 3))
                if k % 2 == 0:
                    P.op("act", lambda e, k=k: e.activation(out=yTs[:, k, :], in_=psb[:, 0:512], func=AF.Copy), reads=["psb"], writes=["yTs"])
                else:
                    P.op("dve", lambda e, k=k: e.tensor_copy(out=yTs[:, k, :], in_=psb[:, 0:512]), reads=["psb"], writes=["yTs"])
            ob0 = (blk - 4) * TB
            P.dma("sp", yT_d[:, ob0:ob0 + TB].rearrange("(k p) t -> p k t", p=128), yTs[:], reads=["yTs"])
    while cstate["next"] < len(ctasks) or cstate["pending"]:
        conv_step()


def build_dense(nc, P, S3, sb, ps, C):
    xT, yT_d, wout_d, wff1_d, wff2_d, outT = C["xT"], C["yT_d"], C["wout_d"], C["wff1_d"], C["wff2_d"], C["outT"]
    mod, sc2p, lnp, onesb = C["mod"], C["sc2p"], C["lnp"], C["onesb"]

    def T(name, shape, dt=F32):
        return sb(name, shape, dt, S3)

    yT = T("yT", [128, 16, TB], BF16)
    z = T("z", [128, 16, TB])
    u2 = yT
    hT = T("hT", [128, 64, TB], BF16)
    xr = [T("xr%d" % i, [128, TB]) for i in range(2)]
    zbs = [T("zb%d" % i, [128, TB], BF16) for i in range(2)]
    zqs = [T("zq%d" % i, [128, TB], BF16) for i in range(2)]
    mean = T("mean", [128, TB])
    rstd = T("rstd", [128, TB])
    tmp = T("tmp", [128, TB])
    w16 = [T("w16_%d" % i, [128, 16, 512], BF16) for i in range(2)]
    w2 = [T("w2_%d" % i, [128, 8, 512], BF16) for i in range(3)]
    AB = T("AB", [128, 64])
    og = [T("og%d" % i, [128, TB]) for i in range(2)]

    lnA = T("lnA", [128, 32])
    P.op("dve", lambda e: e.tensor_scalar(out=lnA[:], in0=lnp[:, 0:32], scalar1=ALPHA, scalar2=None, op0=ALU.mult), reads=["lnp"], writes=["lnA"])
    P.op("dve", lambda e: e.tensor_tensor(out=AB[:, 0:16], in0=lnp[:, 0:16], in1=sc2p[:], op=ALU.mult), reads=["lnp", "sc2p"], writes=["AB"])
    P.op("dve", lambda e: e.tensor_tensor(out=AB[:, 16:32], in0=lnp[:, 16:32], in1=sc2p[:], op=ALU.mult), reads=["lnp", "sc2p", "AB"], writes=["AB"])
    P.op("dve", lambda e: e.tensor_tensor(out=AB[:, 16:32], in0=AB[:, 16:32], in1=mod[:, 48:64], op=ALU.add), reads=["AB", "mod"], writes=["AB"])

    nw16 = [0]
    nw2 = [0]
    nx = [0]
    pb_i = [0]

    def ln_stat_chunk(k):
        zb, zq = zbs[k % 2], zqs[k % 2]
        zbk, zqk = "zb%d" % (k % 2), "zq%d" % (k % 2)
        P.op("act", lambda e, k=k, zb=zb: e.activation(out=zb[:], in_=z[:, k, :], func=AF.Copy), reads=["z%d" % k], writes=[zbk])
        P.op("pool", lambda e, k=k, zq=zq: e.tensor_tensor(out=zq[:], in0=z[:, k, :], in1=z[:, k, :], op=ALU.mult), reads=["z%d" % k], writes=[zqk])
        P.op("pe", lambda e, k=k, zb=zb: e.matmul(ps[4][:, :], lhsT=onesb[:, :], rhs=zb[:], start=(k == 0), stop=(k == 15)),
             reads=[zbk, "onesb"], writes=["ps4"], sig=True)
        P.op("pe", lambda e, k=k, zq=zq: e.matmul(ps[5][:, :], lhsT=onesb[:, :], rhs=zq[:], start=(k == 0), stop=(k == 15)),
             reads=[zqk, "onesb"], writes=["ps5"], sig=True)

    def ln_finish():
        P.op("dve", lambda e: e.tensor_scalar(out=mean[:], in0=ps[4][:, :], scalar1=1.0 / D, scalar2=None, op0=ALU.mult), reads=["ps4"], writes=["mean"])
        P.op("dve", lambda e: e.tensor_tensor(out=tmp[:], in0=mean[:], in1=mean[:], op=ALU.mult), reads=["mean"], writes=["tmp"])
        P.op("dve", lambda e: e.scalar_tensor_tensor(out=rstd[:], in0=ps[5][:, :], scalar=1.0 / D, in1=tmp[:], op0=ALU.mult, op1=ALU.subtract),
             reads=["ps5", "tmp"], writes=["rstd"])
        P.op("dve", lambda e: e.tensor_scalar(out=rstd[:], in0=rstd[:], scalar1=EPS, scalar2=None, op0=ALU.add), reads=["rstd"], writes=["rstd"])
        P.op("act", lambda e: e.activation(out=rstd[:], in_=rstd[:], func=AF.Sqrt), reads=["rstd"], writes=["rstd"])
        P.op("dve", lambda e: e.reciprocal(out=rstd[:], in_=rstd[:]), reads=["rstd"], writes=["rstd"])

    for blk in range(4):
        tsl = slice(blk * TB, (blk + 1) * TB)
        xsl = slice(HALF + blk * TB, HALF + (blk + 1) * TB)
        P.dma("sp", yT[:], yT_d[:, tsl].rearrange("(k p) t -> p k t", p=128), writes=["yT"])
        for dg in range(4):
            w = w16[nw16[0] % 2]
            wk = "w16_%d" % (nw16[0] % 2)
            nw16[0] += 1
            P.dma("pool", w[:], wout_d[:, dg * 512:(dg + 1) * 512].rearrange("(k p) c -> p k c", p=128), writes=[wk])
            for dc in range(4):
                m = dg * 4 + dc
                pb = ps[pb_i[0] % 2]
                pk = "ps%d" % (pb_i[0] % 2)
                pb_i[0] += 1
                for k in range(16):
                    P.op("pe", lambda e, k=k, w=w, dc=dc, pb=pb: e.matmul(pb[:, :], lhsT=w[:, k, dc * 128:(dc + 1) * 128], rhs=yT[:, k, :], start=(k == 0), stop=(k == 15)),
                         reads=[wk, "yT"], writes=[pk], sig=(k == 15))
                x_ = xr[nx[0] % 2]
                xk = "xr%d" % (nx[0] % 2)
                nx[0] += 1
                P.dma("sp", x_[:], xT[m * 128:(m + 1) * 128, xsl], writes=[xk])
                P.op("act", lambda e, x_=x_: e.activation(out=x_[:], in_=x_[:], func=AF.Copy, scale=ALPHA), reads=[xk], writes=[xk])
                P.op("dve", lambda e, pb=pb, m=m, x_=x_: e.scalar_tensor_tensor(out=z[:, m, :], in0=pb[:, :], scalar=mod[:, 32 + m:33 + m], in1=x_[:],
                                                                             op0=ALU.mult, op1=ALU.add), reads=[pk, xk, "mod"], writes=["z%d" % m])
                ln_stat_chunk(m)
        ln_finish()
        def ne(k):
            return "pool" if k % 3 == 2 else "dve"
        for k in range(16):
            P.op(ne(k), lambda e, k=k: e.tensor_tensor(out=z[:, k, :], in0=z[:, k, :], in1=mean[:], op=ALU.subtract), reads=["z%d" % k, "mean"], writes=["z%d" % k])
        for k in range(16):
            P.op(ne(k), lambda e, k=k: e.tensor_tensor(out=z[:, k, :], in0=z[:, k, :], in1=rstd[:], op=ALU.mult), reads=["z%d" % k, "rstd"], writes=["z%d" % k])
        for k in range(16):
            P.op("act", lambda e, k=k: e.activation(out=u2[:, k, :], in_=z[:, k, :], func=AF.Identity, scale=AB[:, k:k + 1], bias=AB[:, 16 + k:17 + k]),
                 reads=["z%d" % k, "AB"], writes=["yT"])
        for k in range(16):
            P.op("dve", lambda e, k=k: e.tensor_scalar(out=z[:, k, :], in0=z[:, k, :], scalar1=lnA[:, k:k + 1], scalar2=lnA[:, 16 + k:17 + k], op0=ALU.mult, op1=ALU.add),
                 reads=["z%d" % k, "lnA"], writes=["z%d" % k])
        for fg in range(16):
            w = w16[nw16[0] % 2]
            wk = "w16_%d" % (nw16[0] % 2)
            nw16[0] += 1
            P.dma("sp", w[:], wff1_d[:, fg * 512:(fg + 1) * 512].rearrange("(k p) c -> p k c", p=128), writes=[wk])
            for fc in range(4):
                f = fg * 4 + fc
                pb = ps[pb_i[0] % 2]
                pk = "ps%d" % (pb_i[0] % 2)
                pb_i[0] += 1
                for k in range(16):
                    P.op("pe", lambda e, k=k, w=w, fc=fc, pb=pb: e.matmul(pb[:, :], lhsT=w[:, k, fc * 128:(fc + 1) * 128], rhs=u2[:, k, :], start=(k == 0), stop=(k == 15)),
                         reads=[wk, "yT"], writes=[pk], sig=(k == 15))
                P.op("act", lambda e, pb=pb: e.activation(out=tmp[:], in_=pb[:, :], func=AF.Relu), reads=[pk], writes=["tmp"])
                P.op("dve", lambda e, f=f: e.tensor_tensor(out=hT[:, f, :], in0=tmp[:], in1=tmp[:], op=ALU.mult), reads=["tmp"], writes=["hT"])
        for dq in range(4):
            for f8 in range(8):
                w = w2[nw2[0] % 3]
                wk = "w2_%d" % (nw2[0] % 3)
                nw2[0] += 1
                P.dma("sp", w[:], wff2_d[f8 * 1024:(f8 + 1) * 1024, dq * 512:(dq + 1) * 512].rearrange("(f p) c -> p f c", p=128), writes=[wk])
                for dc in range(4):
                    for fl in range(8):
                        f = f8 * 8 + fl
                        P.op("pe", lambda e, w=w, fl=fl, dc=dc, f=f: e.matmul(ps[dc][:, :], lhsT=w[:, fl, dc * 128:(dc + 1) * 128], rhs=hT[:, f, :],
                                                                           start=(f == 0), stop=(f == 63)),
                             reads=[wk, "hT"], writes=["ps%d" % dc], sig=(fl == 7))
            for dc in range(4):
                m = dq * 4 + dc
                P.op("dve", lambda e, m=m, dc=dc: e.scalar_tensor_tensor(out=z[:, m, :], in0=ps[dc][:, :], scalar=mod[:, 80 + m:81 + m], in1=z[:, m, :],
                                                                        op0=ALU.mult, op1=ALU.add), reads=["ps%d" % dc, "mod", "z%d" % m], writes=["z%d" % m])
            for dc in range(4):
                ln_stat_chunk(dq * 4 + dc)
        ln_finish()
        for k in range(16):
            P.op(ne(k), lambda e, k=k: e.tensor_tensor(out=z[:, k, :], in0=z[:, k, :], in1=mean[:], op=ALU.subtract), reads=["z%d" % k, "mean"], writes=["z%d" % k])
        for k in range(16):
            P.op(ne(k), lambda e, k=k: e.tensor_tensor(out=z[:, k, :], in0=z[:, k, :], in1=rstd[:], op=ALU.mult), reads=["z%d" % k, "rstd"], writes=["z%d" % k])
        for k in range(16):
            o_ = og[k % 2]
            okk = "og%d" % (k % 2)
            P.op("act", lambda e, k=k, o_=o_: e.activation(out=o_[:], in_=z[:, k, :], func=AF.Identity, scale=lnp[:, 32 + k:33 + k], bias=lnp[:, 48 + k:49 + k]),
                 reads=["z%d" % k, "lnp"], writes=[okk])
            P.dma("sp", outT[k * 128:(k + 1) * 128, tsl], o_[:], reads=[okk])


def _consts():
    c = {}
    c["ident_bf"] = np.eye(128, dtype=np.float32).astype(ml_dtypes.bfloat16)
    c["ident_f"] = np.eye(128, dtype=np.float32)
    s = np.arange(128)[:, None]
    t = np.arange(128)[None, :]
    same = (s // 64) == (t // 64)
    c["ucum"] = np.where(same & (s <= t), -1.0 / 16.0, 0.0).astype(np.float32)
    c["urev"] = np.where(same & (s > t), -1.0 / 16.0, 0.0).astype(np.float32)
    c["maskT"] = np.where(same & (s <= t), 1.0, 0.0).astype(np.float32)
    rm = np.ones((4, 512), np.float32)
    rm[:, ::64] = 0.0
    c["rmask"] = rm
    sel = np.zeros((4, 4, 128), np.float32)
    for h in range(4):
        sel[h, h, :] = 1.0
    c["sel"] = np.ascontiguousarray(sel.transpose(1, 0, 2).reshape(4, 512))
    c["ones_bf"] = np.ones((128, 128), np.float32).astype(ml_dtypes.bfloat16)
    return c


def _pk(v):
    return np.ascontiguousarray(np.asarray(v, np.float32).reshape(16, 128).T)


_NC = None


def kernel(x, c, w_ada, b_ada, w_in, gla_w_alpha_up, gla_b_alpha, gla_norm_g, mlstm_conv_w, mlstm_conv_b,
           mlstm_b_i, mlstm_b_f, mlstm_norm_g, w_out, ln1_g, ln1_b, w_ff1, w_ff2, ln2_g, ln2_b):
    global _NC
    f32 = np.float32
    x = np.asarray(x, f32)
    c = np.asarray(c, f32)
    w_in0 = np.asarray(w_in, f32)[0]
    sizes = [512, 512, 1024, 1024, 16, 512, 512, 1024, 1024, 4, 4]
    offs = np.cumsum([0] + sizes)
    gq, gk, gv, gr, ga, mq, mk, mv, mo, mi, mf = [w_in0[:, offs[i]:offs[i + 1]] for i in range(11)]
    w_tm = np.ascontiguousarray(np.concatenate([gv, gr, mv, mo], axis=1))
    small = np.zeros((D, 128), f32)
    small[:, 0:16] = ga
    small[:, 32:36] = mi
    small[:, 64:68] = mf
    w_fm = np.ascontiguousarray(np.concatenate([gq, gk, mq, mk, small], axis=1))
    wa = np.asarray(w_ada, f32)[0].reshape(D, 6, 16, 128).transpose(0, 1, 3, 2).reshape(D, 6 * D)
    wa = np.ascontiguousarray(wa)
    ba = np.ascontiguousarray(np.asarray(b_ada, f32)[0].reshape(6, 16, 128).transpose(2, 0, 1).reshape(128, 96))
    wup = np.zeros((32, 512), f32)
    wup[0:16] = np.asarray(gla_w_alpha_up, f32)[0]
    wup[16] = np.asarray(gla_b_alpha, f32)[0]
    ggla = np.ascontiguousarray(np.broadcast_to(np.asarray(gla_norm_g, f32)[0][None, :], (128, 1024)))
    gml = np.ascontiguousarray(np.broadcast_to(np.asarray(mlstm_norm_g, f32)[0][None, :], (128, 1024)))
    cw = np.asarray(mlstm_conv_w, f32)[0]
    convw = np.ascontiguousarray(cw.reshape(4, 8, 128).transpose(2, 1, 0).reshape(128, 32))
    convb = np.ascontiguousarray(np.asarray(mlstm_conv_b, f32)[0].reshape(8, 128).T)
    b_i = np.ascontiguousarray(np.asarray(mlstm_b_i, f32)[0].reshape(4, 1))
    b_f = np.ascontiguousarray(np.asarray(mlstm_b_f, f32)[0].reshape(4, 1))
    ln = np.ascontiguousarray(np.concatenate([_pk(ln1_g[0]), _pk(ln1_b[0]), _pk(ln2_g[0]), _pk(ln2_b[0])], axis=1))
    consts = _consts()
    shared = dict(w_ada=wa, b_ada=ba, w_tm=w_tm, w_fm=w_fm, w_up=wup, g_gla=ggla, g_ml=gml, convw=convw, convb=convb,
                  b_i=b_i, b_f=b_f, w_out=np.ascontiguousarray(np.asarray(w_out, f32)[0]),
                  w_ff1=np.ascontiguousarray(np.asarray(w_ff1, f32)[0]), w_ff2=np.ascontiguousarray(np.asarray(w_ff2, f32)[0]), ln=ln)
    shared.update(consts)
    in_maps = []
    for core in range(8):
        b, j = core // 2, core % 2
        xt = np.ascontiguousarray(x[b].T)
        if j == 0:
            xTc = np.concatenate([np.zeros((D, HALF), f32), xt[:, :HALF]], axis=1)
        else:
            xTc = xt
        m = dict(shared)
        m["xT"] = np.ascontiguousarray(xTc)
        m["flag"] = np.full((128, 1), float(j), f32)
        m["cT"] = _pk(c[b])
        in_maps.append(m)
    if _NC is None:
        _NC = build_nc()
    res = run_bass_kernel_spmd(_NC, in_maps, core_ids=list(range(8)), **({'trace': True} if TRACE else {}))
    out = np.empty((NB, SEQ, D), f32)
    for core in range(8):
        b, j = core // 2, core % 2
        out[b, j * HALF:(j + 1) * HALF, :] = np.asarray(res.results[core]["outT"]).T
    kernel.last_results = res
    return out
```

```python
import numpy as np
import ml_dtypes
from contextlib import ExitStack
import concourse.bass as bass
import concourse.mybir as mybir
from concourse.bass_utils import run_bass_kernel_spmd

F32 = mybir.dt.float32
BF16 = mybir.dt.bfloat16
AF = mybir.ActivationFunctionType
ALU = mybir.AluOpType
AX = mybir.AxisListType

D = 2048
SEQ = 4096
NB = 4
HALF = 2048
TB = 512
NTB_ALL = 8
DFF = 8192
ALPHA = 2.0 ** 0.25
EPS = 1e-5
N_TM = 4096
N_FM = 17 * 128
DBG = False
STOP = 99
TRACE = False


class Prog:
    def __init__(self, nc, n_dma_sems=24):
        self.nc = nc
        self.eng = {"pe": nc.tensor, "act": nc.scalar, "dve": nc.vector, "pool": nc.gpsimd, "sp": nc.sync}
        self.sem = {e: nc.alloc_semaphore("s_" + e) for e in ("pe", "act", "dve", "pool")}
        self.cnt = {e: 0 for e in self.sem}
        self.seen = {e: {} for e in self.eng}
        self.lastw = {}
        self.readers = {}
        self.pend = {e: ([], []) for e in self.sem}
        self.dsem = [nc.alloc_semaphore("d%d" % i) for i in range(n_dma_sems)]
        self.dcnt = [0] * n_dma_sems
        self.drr = 0
        self.rec = None

    def record(self, lst):
        self.rec = lst

    def play(self, streams):
        self.rec = None
        idx = [0] * len(streams)
        live = True
        while live:
            live = False
            for i, st in enumerate(streams):
                if idx[i] < len(st):
                    kind, a, k = st[idx[i]]
                    idx[i] += 1
                    live = True
                    (self.op if kind == "op" else self.dma)(*a, **k)

    def _sem(self, key):
        return self.dsem[key[1]] if isinstance(key, tuple) else self.sem[key]

    def _wait(self, e, dep):
        key, val = dep
        if val <= 0 or self.seen[e].get(key, 0) >= val:
            return
        self.seen[e][key] = val
        self.eng[e].wait_ge(self._sem(key), val)

    def _deps(self, e, reads, writes):
        deps = []
        for k in reads:
            if k in self.lastw:
                deps.append(self.lastw[k])
        for k in writes:
            if k in self.lastw:
                deps.append(self.lastw[k])
            deps += self.readers.get(k, [])
        for d in deps:
            if d[0] == "pe" and e == "pe":
                continue
            self._wait(e, d)

    def _mark(self, me, reads, writes):
        for k in reads:
            self.readers.setdefault(k, []).append(me)
        for k in writes:
            self.lastw[k] = me
            self.readers[k] = []

    def op(self, e, fn, reads=(), writes=(), sig=True):
        if self.rec is not None:
            self.rec.append(("op", (e, fn), dict(reads=list(reads), writes=list(writes), sig=sig)))
            return
        writes = list(writes) + [k for k in reads if k.startswith("ps")]
        reads = [k for k in reads if not k.startswith("ps")]
        self._deps(e, reads, writes)
        ins = fn(self.eng[e])
        if not sig:
            self.pend[e][0].extend(reads)
            self.pend[e][1].extend(writes)
            return
        self.cnt[e] += 1
        ins.then_inc(self.sem[e], 1)
        pr, pw = self.pend[e]
        self._mark((e, self.cnt[e]), reads + pr, writes + pw)
        self.pend[e] = ([], [])

    def dma(self, q, out, in_, reads=(), writes=()):
        if self.rec is not None:
            self.rec.append(("dma", (q, out, in_), dict(reads=list(reads), writes=list(writes))))
            return
        self._deps(q, reads, writes)
        i = self.drr
        self.drr = (self.drr + 1) % len(self.dsem)
        self._wait(q, (("d", i), self.dcnt[i]))
        self.dcnt[i] += 16
        self.eng[q].dma_start(out=out, in_=in_).then_inc(self.dsem[i], 16)
        self._mark((("d", i), self.dcnt[i]), list(reads), list(writes))

    def barrier(self):
        for e in self.eng:
            for f in self.sem:
                if f != e:
                    self._wait(e, (f, self.cnt[f]))
            for i in range(len(self.dsem)):
                self._wait(e, (("d", i), self.dcnt[i]))
        self.lastw = {}
        self.readers = {}


def build_nc():
    nc = bass.Bass("TRN2", target_bir_lowering=False)
    P = Prog(nc)

    def din(name, shape, dt=F32):
        return nc.dram_tensor(name, list(shape), dt, kind="ExternalInput").ap()

    xT = din("xT", [D, SEQ])
    flag_d = din("flag", [128, 1])
    cT_d = din("cT", [128, 16])
    wada_d = din("w_ada", [D, 6 * D])
    bada_d = din("b_ada", [128, 96])
    wtm_d = din("w_tm", [D, N_TM])
    wfm_d = din("w_fm", [D, N_FM])
    wup_d = din("w_up", [32, 512])
    ggla_d = din("g_gla", [128, 1024])
    gml_d = din("g_ml", [128, 1024])
    convw_d = din("convw", [128, 32])
    convb_d = din("convb", [128, 8])
    bi_d = din("b_i", [4, 1])
    bf_d = din("b_f", [4, 1])
    wout_d = din("w_out", [D, D])
    wff1_d = din("w_ff1", [D, DFF])
    wff2_d = din("w_ff2", [DFF, D])
    ln_d = din("ln", [128, 64])
    identb_d = din("ident_bf", [128, 128], BF16)
    identf_d = din("ident_f", [128, 128])
    ucum_d = din("ucum", [128, 128])
    urev_d = din("urev", [128, 128])
    maskT_d = din("maskT", [128, 128])
    rmask_d = din("rmask", [4, 512])
    sel_d = din("sel", [4, 512])
    onesb_d = din("ones_bf", [128, 128], BF16)
    outT = nc.dram_tensor("outT", [D, HALF], F32, kind="ExternalOutput").ap()

    skind = "ExternalOutput" if DBG else "Internal"
    P_tm = nc.dram_tensor("P_tm", [SEQ, N_TM], BF16, kind=skind).ap()
    P_fm = nc.dram_tensor("P_fm", [N_FM, SEQ], F32, kind=skind).ap()
    yT_d = nc.dram_tensor("yT_d", [D, HALF], BF16, kind=skind).ap()
    mod_d = nc.dram_tensor("mod_d", [1, 6 * D], F32, kind=skind).ap()
    wff1_b = nc.dram_tensor("wff1_b", [D, DFF], BF16, kind="Internal").ap()
    wff2_b = nc.dram_tensor("wff2_b", [DFF, D], BF16, kind="Internal").ap()

    with ExitStack() as G:
        def sb(name, shape, dt=F32, st=G):
            return st.enter_context(nc.sbuf_tensor("sb_" + name, list(shape), dt))

        ps = [G.enter_context(nc.psum_tensor("ps%d" % i, [128, 512], F32)) for i in range(7)]
        psb = G.enter_context(nc.psum_tensor("psb", [128, 1024], BF16))

        identb = sb("identb", [128, 128], BF16)
        identf = sb("identf", [128, 128])
        ucum = sb("ucum", [128, 128])
        urev = sb("urev", [128, 128])
        maskT = sb("maskT", [128, 128])
        rmask = sb("rmask", [4, 512])
        sel = sb("sel", [4, 512])
        onesb = sb("onesb", [128, 128], BF16)
        flag = sb("flag", [128, 1])
        cT = sb("cT", [128, 16])
        condb = sb("condb", [128, 16], BF16)
        bada = sb("bada", [128, 96])
        mod = sb("mod", [128, 96])
        sc1p = sb("sc1p", [128, 16])
        sc2p = sb("sc2p", [128, 16])
        lnp = sb("lnp", [128, 64])
        wup = sb("wup", [32, 512])
        convw = sb("convw", [128, 32])
        convb = sb("convb", [128, 8])
        b_i = sb("b_i", [4, 1])
        b_f = sb("b_f", [4, 1])
        nb_f = sb("nb_f", [4, 1])
        for t, d_, nm in ((identb, identb_d, "identb"), (identf, identf_d, "identf"), (ucum, ucum_d, "ucum"),
                          (urev, urev_d, "urev"), (maskT, maskT_d, "maskT"), (rmask, rmask_d, "rmask"),
                          (sel, sel_d, "sel"), (onesb, onesb_d, "onesb"), (flag, flag_d, "flag"), (cT, cT_d, "cT"),
                          (bada, bada_d, "bada"), (lnp, ln_d, "lnp"), (wup, wup_d, "wup"),
                          (convw, convw_d, "convw"), (convb, convb_d, "convb"),
                          (b_i, bi_d, "b_i"), (b_f, bf_d, "b_f")):
            P.dma("sp", t[:], d_, writes=[nm])
        P.op("dve", lambda e: e.tensor_scalar(out=nb_f[:], in0=b_f[:], scalar1=-1.0, scalar2=None, op0=ALU.mult),
             reads=["b_f"], writes=["nb_f"])

        def mod_groups(g0, g1, st):
            row = sb("modrow%d" % g0, [1, (g1 - g0) * 512], F32, st)
            wb = [sb("wada%d_%d" % (g0, i), [128, 16, 512], BF16, st) for i in range(2)]
            for g in range(g0, g1):
                w = wb[g % 2]
                wk = "wada%d" % (g % 2)
                P.dma("pool", w[:], wada_d[:, g * 512:(g + 1) * 512].rearrange("(k p) c -> p k c", p=128), writes=[wk])
                for k in range(16):
                    P.op("pe", lambda e, k=k, w=w: e.matmul(ps[0][0:1, :], lhsT=condb[:, k:k + 1], rhs=w[:, k, :],
                                                         start=(k == 0), stop=(k == 15)),
                         reads=[wk, "condb"], writes=["ps0"], sig=(k == 15))
                P.op("dve", lambda e, g=g: e.tensor_copy(out=row[0:1, (g - g0) * 512:(g - g0 + 1) * 512], in_=ps[0][0:1, :]),
                     reads=["ps0"], writes=["modrow"])
            P.dma("sp", mod_d[0:1, g0 * 512:g1 * 512], row[0:1, :], reads=["modrow"], writes=["mod_d"])
            nw = (g1 - g0) // 4
            w0 = g0 // 4
            P.dma("sp", mod[:, w0 * 16:(w0 + nw) * 16].rearrange("p (w k) -> p w k", k=16),
                  mod_d[0:1, g0 * 512:g1 * 512].rearrange("o (w p k) -> p (o w) k", p=128, k=16),
                  reads=["mod_d"], writes=["mod"])
            P.op("dve", lambda e: e.tensor_tensor(out=mod[:, w0 * 16:(w0 + nw) * 16], in0=mod[:, w0 * 16:(w0 + nw) * 16],
                                                  in1=bada[:, w0 * 16:(w0 + nw) * 16], op=ALU.add),
                 reads=["mod", "bada"], writes=["mod"])

        P.op("act", lambda e: e.activation(out=condb[:], in_=cT[:], func=AF.Silu), reads=["cT"], writes=["condb"])
        with ExitStack() as S0:
            mod_groups(0, 8, S0)
            P.op("dve", lambda e: e.tensor_scalar(out=sc1p[:], in0=mod[:, 16:32], scalar1=1.0, scalar2=None, op0=ALU.add),
                 reads=["mod"], writes=["sc1p"])
            P.barrier()

        with ExitStack() as S1:
            uT = sb("uT", [128, 16, SEQ], BF16, S1)
            xs = [sb("xs%d" % i, [128, 2, TB], F32, S1) for i in range(2)]
            wtm = [sb("wtm%d" % i, [128, 16, 512], BF16, S1) for i in range(2)]
            wfm = [sb("wfm%d" % i, [128, 16, 128], BF16, S1) for i in range(2)]
            stg_tm = [sb("stgtm%d" % i, [128, 4, 512], BF16, S1) for i in range(2)]
            stg_fm = [sb("stgfm%d" % i, [128, 512], F32, S1) for i in range(2)]
            nxs = [0]

            def modulate(tb):
                for kq in range(8):
                    x_ = xs[nxs[0] % 2]
                    xk = "xs%d" % (nxs[0] % 2)
                    nxs[0] += 1
                    P.dma("sp", x_[:], xT[kq * 256:(kq + 1) * 256, tb * TB:(tb + 1) * TB].rearrange("(k p) t -> p k t", p=128),
                          writes=[xk])
                    for kk in range(2):
                        k = kq * 2 + kk
                        P.op("act", lambda e, k=k, kk=kk, x_=x_, tb=tb: e.activation(
                            out=uT[:, k, tb * TB:(tb + 1) * TB], in_=x_[:, kk, :], func=AF.Identity,
                            scale=sc1p[:, k:k + 1], bias=mod[:, k:k + 1]),
                            reads=[xk, "sc1p", "mod"], writes=["uT%d" % tb])

            pending_mod = [4, 5, 6, 7, 0, 1, 2, 3]
            modulate(pending_mod.pop(0))
            ev = 0
            pbank = 0
            wad = [sb("wad%d" % i, [128, 16, 128], BF16, S1) for i in range(2)]
            mrow = [sb("mrow%d" % i, [1, 128], F32, S1) for i in range(2)]
            mstate = {"g": 32}

            def mod_step():
                g = mstate["g"]
                if g >= 96:
                    return
                mstate["g"] = g + 1
                w = wad[g % 2]
                wk = "wad%d" % (g % 2)
                r = mrow[g % 2]
                rk = "mrow%d" % (g % 2)
                P.dma("pool", w[:], wada_d[:, g * 128:(g + 1) * 128].rearrange("(k p) c -> p k c", p=128), writes=[wk])
                for k in range(16):
                    P.op("pe", lambda e, k=k, w=w: e.matmul(ps[4][0:1, 0:128], lhsT=condb[:, k:k + 1], rhs=w[:, k, :],
                                                         start=(k == 0), stop=(k == 15)),
                         reads=[wk, "condb"], writes=["ps4"], sig=(k == 15))
                P.op("dve", lambda e, r=r: e.tensor_copy(out=r[0:1, :], in_=ps[4][0:1, 0:128]), reads=["ps4"], writes=[rk])
                P.dma("sp", mod_d[0:1, g * 128:(g + 1) * 128], r[0:1, :], reads=[rk])
            for gi, g in enumerate((2, 3, 6, 7, 0, 1, 4, 5)):
                w = wtm[gi % 2]
                wk = "wtm%d" % (gi % 2)
                P.dma("pool", w[:], wtm_d[:, g * 512:(g + 1) * 512].rearrange("(k p) c -> p k c", p=128), writes=[wk])
                need_prefix = g in (0, 1, 4, 5)
                for tb in (4, 5, 6, 7, 0, 1, 2, 3):
                    if tb < 4 and not need_prefix:
                        continue
                    if pending_mod:
                        modulate(pending_mod.pop(0))
                    sg = stg_tm[ev % 2]
                    sk = "stgtm%d" % (ev % 2)
                    ev += 1
                    for tt in range(4):
                        pb = ps[pbank % 4]
                        pk = "ps%d" % (pbank % 4)
                        pbank += 1
                        t0 = tb * TB + tt * 128
                        for k in range(16):
                            P.op("pe", lambda e, k=k, pb=pb, t0=t0, w=w: e.matmul(
                                pb[:, :], lhsT=uT[:, k, t0:t0 + 128], rhs=w[:, k, :], start=(k == 0), stop=(k == 15)),
                                reads=[wk, "uT%d" % tb], writes=[pk], sig=(k == 15))
                        if tt % 2 == 0:
                            P.op("act", lambda e, pb=pb, sg=sg, tt=tt: e.activation(out=sg[:, tt, :], in_=pb[:, :], func=AF.Copy),
                                 reads=[pk], writes=[sk])
                        else:
                            P.op("dve", lambda e, pb=pb, sg=sg, tt=tt: e.tensor_copy(out=sg[:, tt, :], in_=pb[:, :]),
                                 reads=[pk], writes=[sk])
                    P.dma("sp", P_tm[tb * TB:(tb + 1) * TB, g * 512:(g + 1) * 512].rearrange("(t p) c -> p t c", p=128),
                          sg[:], reads=[sk])
                    mod_step()
            for g in range(17):
                w = wfm[g % 2]
                wk = "wfm%d" % (g % 2)
                P.dma("pool", w[:], wfm_d[:, g * 128:(g + 1) * 128].rearrange("(k p) c -> p k c", p=128), writes=[wk])
                need_prefix = (4 <= g < 8) or g >= 12
                for tb in range(NTB_ALL):
                    if tb < 4 and not need_prefix and not (8 <= g < 12 and tb == 3):
                        continue
                    sg = stg_fm[ev % 2]
                    sk = "stgfm%d" % (ev % 2)
                    ev += 1
                    pb = ps[pbank % 4]
                    pk = "ps%d" % (pbank % 4)
                    pbank += 1
                    for k in range(16):
                        P.op("pe", lambda e, k=k, pb=pb, tb=tb, w=w: e.matmul(
                            pb[:, :], lhsT=w[:, k, :], rhs=uT[:, k, tb * TB:(tb + 1) * TB], start=(k == 0), stop=(k == 15)),
                            reads=[wk, "uT%d" % tb], writes=[pk], sig=(k == 15))
                    if ev % 2 == 0:
                        P.op("act", lambda e, pb=pb, sg=sg: e.activation(out=sg[:], in_=pb[:, :], func=AF.Copy),
                             reads=[pk], writes=[sk])
                    else:
                        P.op("dve", lambda e, pb=pb, sg=sg: e.tensor_copy(out=sg[:], in_=pb[:, :]), reads=[pk], writes=[sk])
                    P.dma("sp", P_fm[g * 128:(g + 1) * 128, tb * TB:(tb + 1) * TB], sg[:], reads=[sk])
                    mod_step()
            while mstate["g"] < 96:
                mod_step()
            P.barrier()

        if STOP <= 1:
            return nc
        P.dma("sp", mod[:, 32:96].rearrange("p (w k) -> p w k", k=16),
              mod_d[0:1, 4096:12288].rearrange("o (w p k) -> p (o w) k", p=128, k=16), writes=["mod"])
        P.op("dve", lambda e: e.tensor_tensor(out=mod[:, 32:96], in0=mod[:, 32:96], in1=bada[:, 32:96], op=ALU.add),
             reads=["mod", "bada"], writes=["mod"])
        P.op("dve", lambda e: e.tensor_scalar(out=sc2p[:], in0=mod[:, 64:80], scalar1=1.0, scalar2=None, op0=ALU.add),
             reads=["mod"], writes=["sc2p"])
        P.barrier()
        if STOP <= 1.5:
            return nc
        with ExitStack() as S2:
            build_mixer(nc, P, S2, sb, ps, psb, dict(
                P_tm=P_tm, P_fm=P_fm, yT_d=yT_d, identb=identb, identf=identf, ucum=ucum, urev=urev, maskT=maskT,
                rmask=rmask, sel=sel, flag=flag, wup=wup, ggla_d=ggla_d, gml_d=gml_d, convw=convw, convb=convb,
                b_i=b_i, nb_f=nb_f, wff1_d=wff1_d, wff2_d=wff2_d, wff1_b=wff1_b, wff2_b=wff2_b))
            P.barrier()

        if STOP <= 2:
            return nc
        with ExitStack() as S3:
            build_dense(nc, P, S3, sb, ps, dict(
                xT=xT, yT_d=yT_d, wout_d=wout_d, wff1_d=wff1_b, wff2_d=wff2_b, outT=outT, mod=mod, sc2p=sc2p,
                lnp=lnp, onesb=onesb))
            P.barrier()
    return nc


def build_mixer(nc, P, S2, sb, ps, psb, C):
    P_tm, P_fm, yT_d = C["P_tm"], C["P_fm"], C["yT_d"]
    identb, identf, ucum, urev, maskT = C["identb"], C["identf"], C["ucum"], C["urev"], C["maskT"]
    rmask, sel, flag, wup = C["rmask"], C["sel"], C["flag"], C["wup"]
    convw, convb, b_i, nb_f = C["convw"], C["convb"], C["b_i"], C["nb_f"]
    GK = 128 ** -0.5

    def T(name, shape, dt=F32):
        return sb(name, shape, dt, S2)

    aT = T("aT", [32, TB])
    iT = T("iT", [4, TB])
    fT = T("fT", [4, TB])
    gq = [T("gq%d" % h, [128, TB]) for h in range(4)]
    gk = [T("gk%d" % h, [128, TB]) for h in range(4)]
    mqp = [T("mqp%d" % h, [128, TB + 3]) for h in range(4)]
    mkp = [T("mkp%d" % h, [128, TB + 3]) for h in range(4)]
    gv = T("gv", [128, 4, 1024], BF16)
    gr = T("gr", [128, 1024], BF16)
    vaug = T("vaug", [128, 4, 4, 257], BF16)
    mo = T("mo", [128, 1024], BF16)
    e1 = T("e1", [128, 512])
    sp = T("sp", [128, 4, 512])
    ecums = [T("ecum%d" % i, [128, TB]) for i in range(2)]
    encum = T("encum", [128, TB])
    erevs = [T("erev%d" % i, [128, TB]) for i in range(2)]
    qe = [T("qe%d" % h, [128, TB], BF16) for h in range(4)]
    ke = [T("ke%d" % h, [128, TB], BF16) for h in range(4)]
    kend = [T("kend%d" % h, [128, TB], BF16) for h in range(4)]
    dec = T("dec", [128, 4, 8])
    kend_tm = T("kend_tm", [128, 4, 4, 128], BF16)
    mk_tm = T("mk_tm", [128, 4, 4, 128], BF16)
    cacc = T("cacc", [128, TB])
    qc = [T("qc%d" % h, [128, TB], BF16) for h in range(4)]
    kc = [T("kc%d" % h, [128, TB], BF16) for h in range(4)]
    spf = T("spf", [4, TB])
    gpl = T("gpl", [4, TB])
    gpl2 = T("gpl2", [4, TB])
    am = T("am", [4, TB])
    amax = T("amax", [4, 8])
    Mn = T("Mn", [4, 8])
    mprev_all = T("mprev_all", [4, 9])
    dlog = T("dlog", [4, 8])
    wklog = am
    thrlog = gpl
    wkthr = T("wkthr", [128, 4, 8])
    wIb = T("wIb", [128, 32])
    vp = vaug
    scT = [T("scT%d" % i, [128, 128], BF16) for i in range(4)]
    Sg = [T("Sg%d" % h, [128, 256]) for h in range(4)]
    Sgb = [T("Sgb%d" % h, [128, 256], BF16) for h in range(4)]
    Cm = [T("Cm%d" % h, [128, 257]) for h in range(4)]
    Csb = [T("Csb%d" % h, [128, 257], BF16) for h in range(4)]
    silr = T("silr", [128, 1024], BF16)
    sigo = T("sigo", [128, 1024], BF16)
    wk4 = [T("wk4_%d" % h, [128, 256]) for h in range(4)]
    ss = T("ss", [128, 4])
    dn = T("dn", [128, 4])
    rs4 = T("rs4", [128, 4])
    bst = T("bst", [128, 4, 6])
    mv2 = T("mv2", [128, 4, 2])
    sm = T("sm", [128, 4])
    y_tm = T("y_tm", [128, 4, 2048], BF16)
    yTs = T("yTs", [128, 16, TB], BF16)

    ggla = T("ggla", [128, 1024])
    gml = T("gml", [128, 1024])
    P.dma("sp", ggla[:], C["ggla_d"], writes=["ggla"])
    P.dma("sp", gml[:], C["gml_d"], writes=["gml"])
    cstg = [T("cstg%d" % i, [128, 16, 256], BF16) for i in range(2)]
    ctasks = []
    for cg in range(32):
        ctasks.append((C["wff1_d"][:, cg * 256:(cg + 1) * 256].rearrange("(k p) c -> p k c", p=128),
                       C["wff1_b"][:, cg * 256:(cg + 1) * 256].rearrange("(k p) c -> p k c", p=128)))
    for fq in range(4):
        for cg in range(8):
            ctasks.append((C["wff2_d"][fq * 2048:(fq + 1) * 2048, cg * 256:(cg + 1) * 256].rearrange("(k p) c -> p k c", p=128),
                           C["wff2_b"][fq * 2048:(fq + 1) * 2048, cg * 256:(cg + 1) * 256].rearrange("(k p) c -> p k c", p=128)))
    cstate = {"next": 0, "pending": []}

    def conv_step():
        for (i, dst) in cstate["pending"]:
            P.dma("sp", dst, cstg[i][:], reads=["cstg%d" % i])
        cstate["pending"] = []
        for i in range(2):
            if cstate["next"] < len(ctasks):
                src, dst = ctasks[cstate["next"]]
                cstate["next"] += 1
                P.dma("pool", cstg[i][:], src, writes=["cstg%d" % i])
                cstate["pending"].append((i, dst))

    P.op("pool", lambda e: e.memset(aT[:], 1.0), writes=["aT"])
    P.op("pool", lambda e: e.memset(vaug[:].rearrange("p a b c -> p (a b c)"), 1.0), writes=["vaug"])
    P.op("pool", lambda e: e.memset(mprev_all[:], 0.0), writes=["mprev_all"])
    for h in range(4):
        P.op("pool", lambda e, h=h: e.memset(Sg[h][:], 0.0), writes=["Sg%d" % h])
        P.op("pool", lambda e, h=h: e.memset(Sgb[h][:], 0.0), writes=["Sgb%d" % h])
        P.op("pool", lambda e, h=h: e.memset(Cm[h][:], 0.0), writes=["Cm%d" % h])
        P.op("pool", lambda e, h=h: e.memset(mqp[h][:, 0:3], 0.0), writes=["mqp%d" % h])
        P.op("pool", lambda e, h=h: e.memset(mkp[h][:, 0:3], 0.0), writes=["mkp%d" % h])

    for blk in range(NTB_ALL):
        own = blk >= 4
        t0 = blk * TB
        tsl = slice(t0, t0 + TB)
        P.dma("sp", aT[0:16, :], P_fm[16 * 128:16 * 128 + 16, tsl], writes=["aT"])
        P.dma("sp", iT[:], P_fm[16 * 128 + 32:16 * 128 + 36, tsl], writes=["iT"])
        P.dma("sp", fT[:], P_fm[16 * 128 + 64:16 * 128 + 68, tsl], writes=["fT"])
        for h in range(4):
            P.dma("sp", gk[h][:], P_fm[(4 + h) * 128:(5 + h) * 128, tsl], writes=["gk%d" % h])
            if own:
                P.dma("sp", gq[h][:], P_fm[h * 128:(h + 1) * 128, tsl], writes=["gq%d" % h])
            if blk == 0:
                P.dma("sp", mkp[h][:, 3:], P_fm[(12 + h) * 128:(13 + h) * 128, tsl], writes=["mkp%d" % h])
            else:
                P.dma("sp", mkp[h][:], P_fm[(12 + h) * 128:(13 + h) * 128, t0 - 3:t0 + TB], writes=["mkp%d" % h])
                if own:
                    P.dma("sp", mqp[h][:], P_fm[(8 + h) * 128:(9 + h) * 128, t0 - 3:t0 + TB], writes=["mqp%d" % h])
            if blk == 4:
                P.op("dve", lambda e, h=h: e.tensor_scalar(out=mkp[h][:, 0:3], in0=mkp[h][:, 0:3], scalar1=flag[:, 0:1],
                                                           scalar2=None, op0=ALU.mult), reads=["mkp%d" % h, "flag"], writes=["mkp%d" % h])
                P.op("dve", lambda e, h=h: e.tensor_scalar(out=mqp[h][:, 0:3], in0=mqp[h][:, 0:3], scalar1=flag[:, 0:1],
                                                           scalar2=None, op0=ALU.mult), reads=["mqp%d" % h, "flag"], writes=["mqp%d" % h])
        rows = P_tm[tsl, :].rearrange("(t p) c -> p t c", p=128)
        P.dma("sp", gv[:], rows[:, :, 0:1024], writes=["gv"])
        for h in range(4):
            P.dma("sp", vaug[:, :, h, 0:256], rows[:, :, 2048 + h * 256:2048 + (h + 1) * 256], writes=["vaug"])
        P.op("pool", lambda e: e.memset(vaug[:, :, :, 256:257], 1.0), writes=["vaug"])
        if blk == 4:
            for h in range(4):
                P.op("dve", lambda e, h=h: e.tensor_scalar(out=Sg[h][:], in0=Sg[h][:], scalar1=flag[:, 0:1], scalar2=None, op0=ALU.mult),
                     reads=["Sg%d" % h, "flag"], writes=["Sg%d" % h])
                P.op("dve", lambda e, h=h: e.tensor_scalar(out=Sgb[h][:], in0=Sgb[h][:], scalar1=flag[:, 0:1], scalar2=None, op0=ALU.mult),
                     reads=["Sgb%d" % h, "flag"], writes=["Sgb%d" % h])
                P.op("dve", lambda e, h=h: e.tensor_scalar(out=Cm[h][:], in0=Cm[h][:], scalar1=flag[:, 0:1], scalar2=None, op0=ALU.mult),
                     reads=["Cm%d" % h, "flag"], writes=["Cm%d" % h])

        stG, stC, stM = [], [], []
        P.record(stG)
        for tt in range(4):
            P.op("pe", lambda e, tt=tt: e.matmul(ps[0][:, :], lhsT=aT[0:32, tt * 128:(tt + 1) * 128], rhs=wup[0:32, :], start=True, stop=True),
                 reads=["aT", "wup"], writes=["ps0"])
            P.op("act", lambda e: e.activation(out=e1[:], in_=ps[0][:, :], func=AF.Exp, scale=-1.0), reads=["ps0"], writes=["e1"])
            P.op("act", lambda e, tt=tt: e.activation(out=sp[:, tt, :], in_=e1[:], func=AF.Ln, bias=1.0), reads=["e1"], writes=["sp"])
        for h in range(4):
            ecum, erev = ecums[h % 2], erevs[h % 2]
            eck, erk = "ecum%d" % (h % 2), "erev%d" % (h % 2)
            for tt in range(4):
                P.op("pe", lambda e, h=h, tt=tt: e.matmul(ps[1][:, tt * 128:(tt + 1) * 128], lhsT=sp[:, tt, h * 128:(h + 1) * 128], rhs=ucum[:, :],
                                                          start=True, stop=True), reads=["sp", "ucum"], writes=["ps1"], sig=(tt == 3))
            for tt in range(4):
                P.op("pe", lambda e, h=h, tt=tt: e.matmul(ps[2][:, tt * 128:(tt + 1) * 128], lhsT=sp[:, tt, h * 128:(h + 1) * 128], rhs=urev[:, :],
                                                          start=True, stop=True), reads=["sp", "urev"], writes=["ps2"], sig=(tt == 3))
            P.op("act", lambda e, ecum=ecum: e.activation(out=ecum[:], in_=ps[1][:, :], func=AF.Exp), reads=["ps1"], writes=[eck])
            P.op("act", lambda e, erev=erev: e.activation(out=erev[:], in_=ps[2][:, :], func=AF.Exp), reads=["ps2"], writes=[erk])
            if own:
                P.op("act", lambda e: e.activation(out=encum[:], in_=ps[1][:, :], func=AF.Exp, scale=-1.0), reads=["ps1"], writes=["encum"])
                P.op("dve", lambda e, h=h, ecum=ecum: e.scalar_tensor_tensor(out=qe[h][:], in0=gq[h][:], scalar=GK, in1=ecum[:], op0=ALU.mult, op1=ALU.mult),
                     reads=["gq%d" % h, eck], writes=["qe%d" % h])
                P.op("dve", lambda e, h=h: e.tensor_tensor(out=ke[h][:], in0=gk[h][:], in1=encum[:], op=ALU.mult),
                     reads=["gk%d" % h, "encum"], writes=["ke%d" % h])
            P.op("dve", lambda e, h=h, erev=erev: e.tensor_tensor(out=kend[h][:], in0=gk[h][:], in1=erev[:], op=ALU.mult),
                 reads=["gk%d" % h, erk], writes=["kend%d" % h])
            P.op("dve", lambda e, h=h, ecum=ecum: e.tensor_copy(out=dec[:, h, :], in_=ecum[:].rearrange("p (n c) -> p n c", c=64)[:, :, 63]),
                 reads=[eck], writes=["dec%d" % h])
        for tt in range(4):
            for h in range(4):
                P.op("pe", lambda e, h=h, tt=tt: e.transpose(psb[:, h * 128:(h + 1) * 128], kend[h][:, tt * 128:(tt + 1) * 128], identb[:, :]),
                     reads=["kend%d" % h, "identb"], writes=["psb"], sig=(h == 3))
            P.op("act", lambda e, tt=tt: e.activation(out=kend_tm[:, tt].rearrange("p h d -> p (h d)"), in_=psb[:, 0:512], func=AF.Copy),
                 reads=["psb"], writes=["kend_tm"])

        P.record(stC)
        for h in range(4):
            for qk, (pre, dst) in enumerate(((mqp[h], qc[h]), (mkp[h], kc[h]))):
                if qk == 0 and not own:
                    continue
                pk = ("mqp%d" if qk == 0 else "mkp%d") % h
                dk_ = ("qc%d" if qk == 0 else "kc%d") % h
                ci = qk * 4 + h
                P.op("dve", lambda e, pre=pre, ci=ci: e.tensor_scalar(out=cacc[:], in0=pre[:, 3:TB + 3], scalar1=convw[:, ci * 4 + 3:ci * 4 + 4],
                                                                      scalar2=convb[:, ci:ci + 1], op0=ALU.mult, op1=ALU.add),
                     reads=[pk, "convw", "convb"], writes=["cacc"])
                for j in (2, 1, 0):
                    P.op("dve", lambda e, pre=pre, ci=ci, j=j: e.scalar_tensor_tensor(out=cacc[:], in0=pre[:, j:TB + j], scalar=convw[:, ci * 4 + j:ci * 4 + j + 1],
                                                                                   in1=cacc[:], op0=ALU.mult, op1=ALU.add),
                         reads=[pk, "convw", "cacc"], writes=["cacc"])
                if qk == 0:
                    P.op("act", lambda e: e.activation(out=cacc[:], in_=cacc[:], func=AF.Silu), reads=["cacc"], writes=["cacc"])
                    P.op("pool", lambda e, dst=dst: e.tensor_scalar(out=dst[:], in0=cacc[:], scalar1=GK, scalar2=0.0, op0=ALU.mult, op1=ALU.add),
                         reads=["cacc"], writes=[dk_])
                else:
                    P.op("act", lambda e, dst=dst: e.activation(out=dst[:], in_=cacc[:], func=AF.Silu), reads=["cacc"], writes=[dk_])
        P.record(stM)
        P.op("act", lambda e: e.activation(out=spf[:], in_=fT[:], func=AF.Exp, scale=-1.0, bias=nb_f[:, 0:1]), reads=["fT", "nb_f"], writes=["spf"])
        P.op("act", lambda e: e.activation(out=spf[:], in_=spf[:], func=AF.Ln, bias=1.0), reads=["spf"], writes=["spf"])
        P.op("pool", lambda e: e.tensor_copy(out=gpl[:], in_=spf[:]), reads=["spf"], writes=["gpl"])
        cur, nxt, ck, nk = gpl, gpl2, "gpl", "gpl2"
        for sh in (1, 2, 4, 8, 16, 32):
            cv = cur[:].rearrange("p (n c) -> p n c", c=64)
            nv = nxt[:].rearrange("p (n c) -> p n c", c=64)
            P.op("dve", lambda e, cv=cv, nv=nv, sh=sh: e.tensor_tensor(out=nv[:, :, sh:64], in0=cv[:, :, sh:64], in1=cv[:, :, 0:64 - sh], op=ALU.add),
                 reads=[ck], writes=[nk])
            P.op("dve", lambda e, cv=cv, nv=nv, sh=sh: e.tensor_copy(out=nv[:, :, 0:sh], in_=cv[:, :, 0:sh]), reads=[ck], writes=[nk])
            cur, nxt, ck, nk = nxt, cur, nk, ck
        P.op("dve", lambda e: e.scalar_tensor_tensor(out=am[:], in0=iT[:], scalar=b_i[:, 0:1], in1=gpl[:], op0=ALU.add, op1=ALU.add),
             reads=["iT", "b_i", "gpl"], writes=["am"])
        P.op("dve", lambda e: e.tensor_reduce(out=amax[:], in_=am[:].rearrange("p (n c) -> p n c", c=64), axis=AX.X, op=ALU.max),
             reads=["am"], writes=["amax"])
        if blk > 0:
            P.op("dve", lambda e: e.tensor_copy(out=mprev_all[:, 0:1], in_=mprev_all[:, 8:9]), reads=["mprev_all"], writes=["mprev_all"])
        if blk == 4:
            P.op("dve", lambda e: e.tensor_scalar(out=mprev_all[:, 0:1], in0=mprev_all[:, 0:1], scalar1=flag[0:4, 0:1], scalar2=None, op0=ALU.mult),
                 reads=["mprev_all", "flag"], writes=["mprev_all"])
        gl = gpl[:].rearrange("p (n c) -> p n c", c=64)
        for n in range(8):
            P.op("dve", lambda e, n=n: e.tensor_tensor(out=Mn[:, n:n + 1], in0=mprev_all[:, n:n + 1], in1=amax[:, n:n + 1], op=ALU.max),
                 reads=["mprev_all", "amax"], writes=["Mn"])
            P.op("dve", lambda e, n=n: e.tensor_tensor(out=mprev_all[:, n + 1:n + 2], in0=Mn[:, n:n + 1], in1=gl[:, n, 63:64], op=ALU.subtract),
                 reads=["Mn", "gpl"], writes=["mprev_all"])
        P.op("dve", lambda e: e.tensor_tensor(out=dlog[:], in0=mprev_all[:, 0:8], in1=Mn[:], op=ALU.subtract),
             reads=["mprev_all", "Mn"], writes=["dlog"])
        Mb = Mn[:].rearrange("p (n o) -> p n o", o=1).broadcast_to([4, 8, 64])
        P.op("dve", lambda e: e.tensor_tensor(out=wklog[:].rearrange("p (n c) -> p n c", c=64), in0=am[:].rearrange("p (n c) -> p n c", c=64),
                                              in1=Mb, op=ALU.subtract), reads=["am", "Mn"], writes=["am"])
        P.op("dve", lambda e: e.tensor_tensor(out=thrlog[:].rearrange("p (n c) -> p n c", c=64), in0=gpl[:].rearrange("p (n c) -> p n c", c=64),
                                              in1=Mb, op=ALU.subtract), reads=["gpl", "Mn"], writes=["gpl"])
        for tt in range(4):
            P.op("pe", lambda e, tt=tt: e.matmul(ps[5][:, tt * 8:tt * 8 + 4], lhsT=wklog[:, tt * 128:(tt + 1) * 128], rhs=identf[0:4, 0:4], start=True, stop=True),
                 reads=["am", "identf"], writes=["ps5"], sig=False)
            P.op("pe", lambda e, tt=tt: e.matmul(ps[5][:, tt * 8 + 4:tt * 8 + 8], lhsT=thrlog[:, tt * 128:(tt + 1) * 128], rhs=identf[0:4, 0:4], start=True, stop=True),
                 reads=["gpl", "identf"], writes=["ps5"], sig=(tt == 3))
        P.op("act", lambda e: e.activation(out=wkthr[:].rearrange("p t c -> p (t c)"), in_=ps[5][:, 0:32], func=AF.Exp), reads=["ps5"], writes=["wkthr"])
        for h in range(4):
            P.op("pe", lambda e, h=h: e.matmul(ps[5][:, 64 + h * 8:64 + h * 8 + 8], lhsT=sel[:, h * 128:(h + 1) * 128], rhs=dlog[:, :], start=True, stop=True),
                 reads=["sel", "dlog"], writes=["ps5"], sig=(h == 3))
        P.op("act", lambda e: e.activation(out=wIb[:], in_=ps[5][:, 64:96], func=AF.Exp), reads=["ps5"], writes=["wIb"])
        for tt in range(4):
            for h in range(4):
                P.op("pool", lambda e, tt=tt, h=h: e.tensor_scalar(out=vp[:, tt, h, :], in0=vaug[:, tt, h, :], scalar1=wkthr[:, tt, h:h + 1], scalar2=0.0,
                                                                  op0=ALU.mult, op1=ALU.add), reads=["vaug", "wkthr"], writes=["vaug"])
        P.record(stC)
        for tt in range(4):
            for h in range(4):
                P.op("pe", lambda e, h=h, tt=tt: e.transpose(psb[:, 512 + h * 128:512 + (h + 1) * 128], kc[h][:, tt * 128:(tt + 1) * 128], identb[:, :]),
                     reads=["kc%d" % h, "identb"], writes=["psb"], sig=(h == 3))
            P.op("act", lambda e, tt=tt: e.activation(out=mk_tm[:, tt].rearrange("p h d -> p (h d)"), in_=psb[:, 512:1024], func=AF.Copy),
                 reads=["psb"], writes=["mk_tm"])

        P.play([stG, stC, stM])
        OB = [ps[1], ps[2], ps[3], ps[4]]
        OBK = ["ps1", "ps2", "ps3", "ps4"]
        DP = [ps[6], ps[0]]
        DPK = ["ps6", "ps0"]
        ndp = 0
        for tt in range(4):
            c0 = tt * 128
            conv_step()
            if own:
                P.dma("sp", gr[:], P_tm[t0 + c0:t0 + c0 + 128, 1024:2048], writes=["gr"])
                P.dma("sp", mo[:], P_tm[t0 + c0:t0 + c0 + 128, 3072:4096], writes=["mo"])
                P.op("act", lambda e: e.activation(out=silr[:], in_=gr[:], func=AF.Silu), reads=["gr"], writes=["silr"])
                P.op("act", lambda e: e.activation(out=sigo[:], in_=mo[:], func=AF.Sigmoid), reads=["mo"], writes=["sigo"])
            if own:
                for h in range(4):
                    P.op("pe", lambda e, h=h, c0=c0: e.matmul(ps[5][:, 0:128], lhsT=ke[h][:, c0:c0 + 128], rhs=qe[h][:, c0:c0 + 128], start=True, stop=True),
                         reads=["ke%d" % h, "qe%d" % h], writes=["ps5"])
                    P.op("dve", lambda e, h=h: e.tensor_tensor(out=scT[h][:], in0=ps[5][:, 0:128], in1=maskT[:], op=ALU.mult),
                         reads=["ps5", "maskT"], writes=["scT%d" % h])
                    P.op("pe", lambda e, h=h, tt=tt: e.matmul(OB[h][:, 0:256], lhsT=scT[h][:], rhs=gv[:, tt, h * 256:(h + 1) * 256], start=True, stop=False),
                         reads=["scT%d" % h, "gv"], writes=[OBK[h]], sig=False)
            for half in range(2):
                n = tt * 2 + half
                r0 = half * 64
                for h in range(4):
                    if own:
                        P.op("pe", lambda e, h=h, c0=c0, r0=r0: e.matmul(OB[h][r0:r0 + 64, 0:256], lhsT=qe[h][:, c0 + r0:c0 + r0 + 64], rhs=Sgb[h][:],
                                                                      start=False, stop=(r0 == 64)),
                             reads=["qe%d" % h, "Sgb%d" % h], writes=[OBK[h]], sig=(half == 1))
                    dp, dpk = DP[ndp % 2], DPK[ndp % 2]
                    ndp += 1
                    P.op("pe", lambda e, h=h, tt=tt, r0=r0, dp=dp: e.matmul(dp[:, 0:256], lhsT=kend_tm[r0:r0 + 64, tt, h, :], rhs=gv[r0:r0 + 64, tt, h * 256:(h + 1) * 256],
                                                                          start=True, stop=True), reads=["kend_tm", "gv"], writes=[dpk])
                    P.op("dve", lambda e, h=h, n=n, dp=dp: e.scalar_tensor_tensor(out=Sg[h][:], in0=Sg[h][:], scalar=dec[:, h, n:n + 1], in1=dp[:, 0:256],
                                                                                op0=ALU.mult, op1=ALU.add), reads=["Sg%d" % h, "dec%d" % h, dpk], writes=["Sg%d" % h])
                    P.op("act", lambda e, h=h: e.activation(out=Sgb[h][:], in_=Sg[h][:], func=AF.Copy), reads=["Sg%d" % h], writes=["Sgb%d" % h])
            if own:
                for h in range(4):
                    P.op("act", lambda e, h=h: e.activation(out=wk4[h][:], in_=OB[h][:, 0:256], func=AF.Square), reads=[OBK[h]], writes=["wk4_%d" % h])
                for h in range(4):
                    P.op("dve", lambda e, h=h: e.tensor_reduce(out=ss[:, h:h + 1], in_=wk4[h][:], axis=AX.X, op=ALU.add),
                         reads=["wk4_%d" % h], writes=["ss%d" % h])
                P.op("dve", lambda e: e.tensor_scalar(out=sm[:, 0:4], in0=ss[:, 0:4], scalar1=1.0 / 256, scalar2=EPS, op0=ALU.mult, op1=ALU.add),
                     reads=["ss0", "ss1", "ss2", "ss3"], writes=["sm"])
                P.op("act", lambda e: e.activation(out=sm[:, 0:4], in_=sm[:, 0:4], func=AF.Sqrt), reads=["sm"], writes=["sm"])
                P.op("dve", lambda e: e.reciprocal(out=sm[:, 0:4], in_=sm[:, 0:4]), reads=["sm"], writes=["sm"])
                for h in range(4):
                    P.op("dve", lambda e, h=h: e.scalar_tensor_tensor(out=wk4[h][:], in0=OB[h][:, 0:256], scalar=sm[:, h:h + 1], in1=ggla[:, h * 256:(h + 1) * 256],
                                                                    op0=ALU.mult, op1=ALU.mult), reads=[OBK[h], "sm", "ggla"], writes=["wk4_%d" % h])
                for h in range(4):
                    P.op("pool", lambda e, tt=tt, h=h: e.tensor_tensor(out=y_tm[:, tt, h * 256:(h + 1) * 256], in0=wk4[h][:], in1=silr[:, h * 256:(h + 1) * 256], op=ALU.mult),
                         reads=["wk4_%d" % h, "silr"], writes=["y_tm"])
            if own:
                for h in range(4):
                    P.op("pe", lambda e, h=h, c0=c0: e.matmul(ps[5][:, 0:128], lhsT=kc[h][:, c0:c0 + 128], rhs=qc[h][:, c0:c0 + 128], start=True, stop=True),
                         reads=["kc%d" % h, "qc%d" % h], writes=["ps5"])
                    P.op("dve", lambda e, h=h: e.tensor_tensor(out=scT[h][:], in0=ps[5][:, 0:128], in1=maskT[:], op=ALU.mult),
                         reads=["ps5", "maskT"], writes=["scT%d" % h])
                    P.op("pe", lambda e, h=h, tt=tt: e.matmul(OB[h][:, 0:257], lhsT=scT[h][:], rhs=vp[:, tt, h, :], start=True, stop=False),
                         reads=["scT%d" % h, "vaug"], writes=[OBK[h]], sig=False)
            for half in range(2):
                n = tt * 2 + half
                r0 = half * 64
                for h in range(4):
                    if own:
                        P.op("pool", lambda e, h=h, n=n: e.tensor_scalar(out=Csb[h][:], in0=Cm[h][:], scalar1=wIb[:, h * 8 + n:h * 8 + n + 1], scalar2=0.0, op0=ALU.mult, op1=ALU.add),
                             reads=["Cm%d" % h, "wIb"], writes=["Csb%d" % h])
                        P.op("pe", lambda e, h=h, c0=c0, r0=r0: e.matmul(OB[h][r0:r0 + 64, 0:257], lhsT=qc[h][:, c0 + r0:c0 + r0 + 64], rhs=Csb[h][:],
                                                                      start=False, stop=(r0 == 64)),
                             reads=["qc%d" % h, "Csb%d" % h], writes=[OBK[h]], sig=(half == 1))
                    dp, dpk = DP[ndp % 2], DPK[ndp % 2]
                    ndp += 1
                    P.op("pe", lambda e, h=h, tt=tt, r0=r0, dp=dp: e.matmul(dp[:, 0:257], lhsT=mk_tm[r0:r0 + 64, tt, h, :], rhs=vp[r0:r0 + 64, tt, h, :],
                                                                          start=True, stop=True), reads=["mk_tm", "vaug"], writes=[dpk])
                    P.op("dve", lambda e, h=h, n=n, dp=dp: e.scalar_tensor_tensor(out=Cm[h][:], in0=Cm[h][:], scalar=wIb[:, h * 8 + n:h * 8 + n + 1], in1=dp[:, 0:257],
                                                                                op0=ALU.mult, op1=ALU.add), reads=["Cm%d" % h, "wIb", dpk], writes=["Cm%d" % h])
            if own:
                for h in range(4):
                    P.op("act", lambda e, h=h: e.activation(out=dn[:, h:h + 1], in_=OB[h][:, 256:257], func=AF.Copy), reads=[OBK[h]], writes=["dn%d" % h])
                P.op("dve", lambda e: e.tensor_scalar(out=ss[:, 0:4], in0=dn[:, 0:4], scalar1=-1.0, scalar2=None, op0=ALU.mult),
                     reads=["dn0", "dn1", "dn2", "dn3", "ss0", "ss1", "ss2", "ss3"], writes=["ssb"])
                P.op("dve", lambda e: e.tensor_tensor(out=ss[:, 0:4], in0=dn[:, 0:4], in1=ss[:, 0:4], op=ALU.max), reads=["ssb", "dn0", "dn1", "dn2", "dn3"], writes=["ssb"])
                P.op("dve", lambda e, tt=tt: e.tensor_tensor(out=sm[:, 0:4], in0=ss[:, 0:4], in1=wkthr[:, tt, 4:8], op=ALU.max), reads=["ssb", "wkthr"], writes=["sm"])
                P.op("dve", lambda e: e.reciprocal(out=sm[:, 0:4], in_=sm[:, 0:4]), reads=["sm"], writes=["sm"])
                for h in range(4):
                    P.op("dve", lambda e, h=h: e.tensor_scalar(out=wk4[h][:], in0=OB[h][:, 0:256], scalar1=sm[:, h:h + 1], scalar2=None, op0=ALU.mult),
                         reads=[OBK[h], "sm"], writes=["wk4_%d" % h])
                for h in range(4):
                    P.op("dve", lambda e, h=h: e.bn_stats(out=bst[:, h, :], in_=wk4[h][:]), reads=["wk4_%d" % h], writes=["bst%d" % h])
                for h in range(4):
                    P.op("dve", lambda e, h=h: e.bn_aggr(out=mv2[:, h, :], in_=bst[:, h, :]), reads=["bst%d" % h], writes=["mv%d" % h])
                P.op("dve", lambda e: e.tensor_scalar(out=rs4[:, 0:4], in0=mv2[:, :, 1], scalar1=EPS, scalar2=None, op0=ALU.add),
                     reads=["mv0", "mv1", "mv2", "mv3"], writes=["rs4"])
                P.op("act", lambda e: e.activation(out=rs4[:, 0:4], in_=rs4[:, 0:4], func=AF.Sqrt), reads=["rs4"], writes=["rs4"])
                P.op("dve", lambda e: e.reciprocal(out=rs4[:, 0:4], in_=rs4[:, 0:4]), reads=["rs4"], writes=["rs4"])
                for h in range(4):
                    P.op("dve", lambda e, h=h: e.tensor_scalar(out=wk4[h][:], in0=wk4[h][:], scalar1=mv2[:, h, 0:1], scalar2=rs4[:, h:h + 1], op0=ALU.subtract, op1=ALU.mult),
                         reads=["wk4_%d" % h, "mv%d" % h, "rs4"], writes=["wk4_%d" % h])
                for h in range(4):
                    P.op("pool", lambda e, h=h: e.tensor_tensor(out=wk4[h][:], in0=wk4[h][:], in1=gml[:, h * 256:(h + 1) * 256], op=ALU.mult),
                         reads=["wk4_%d" % h, "gml"], writes=["wk4_%d" % h])
                for h in range(4):
                    P.op("pool", lambda e, tt=tt, h=h: e.tensor_tensor(out=y_tm[:, tt, 1024 + h * 256:1024 + (h + 1) * 256], in0=wk4[h][:], in1=sigo[:, h * 256:(h + 1) * 256], op=ALU.mult),
                         reads=["wk4_%d" % h, "sigo"], writes=["y_tm"])
        if own:
            for k in range(16):
                for tt in range(4):
                    P.op("pe", lambda e, k=k, tt=tt: e.transpose(psb[:, tt * 128:(tt + 1) * 128], y_tm[:, tt, k * 128:(k + 1) * 128], identb[:, :]),
                         reads=["y_tm", "identb"], writes=["psb"], sig=(tt == 3))
                if k % 2 == 0:
                    P.op("act", lambda e, k=k: e.activation(out=yTs[:, k, :], in_=psb[:, 0:512], func=AF.Copy), reads=["psb"], writes=["yTs"])
                else:
                    P.op("dve", lambda e, k=k: e.tensor_copy(out=yTs[:, k, :], in_=psb[:, 0:512]), reads=["psb"], writes=["yTs"])
            ob0 = (blk - 4) * TB
            P.dma("sp", yT_d[:, ob0:ob0 + TB].rearrange("(k p) t -> p k t", p=128), yTs[:], reads=["yTs"])
    while cstate["next"] < len(ctasks) or cstate["pending"]:
        conv_step()


def build_dense(nc, P, S3, sb, ps, C):
    xT, yT_d, wout_d, wff1_d, wff2_d, outT = C["xT"], C["yT_d"], C["wout_d"], C["wff1_d"], C["wff2_d"], C["outT"]
    mod, sc2p, lnp, onesb = C["mod"], C["sc2p"], C["lnp"], C["onesb"]

    def T(name, shape, dt=F32):
        return sb(name, shape, dt, S3)

    yT = T("yT", [128, 16, TB], BF16)
    z = T("z", [128, 16, TB])
    u2 = yT
    hT = T("hT", [128, 64, TB], BF16)
    xr = [T("xr%d" % i, [128, TB]) for i in range(2)]
    zbs = [T("zb%d" % i, [128, TB], BF16) for i in range(2)]
    zqs = [T("zq%d" % i, [128, TB], BF16) for i in range(2)]
    mean = T("mean", [128, TB])
    rstd = T("rstd", [128, TB])
    tmp = T("tmp", [128, TB])
    w16 = [T("w16_%d" % i, [128, 16, 512], BF16) for i in range(2)]
    w2 = [T("w2_%d" % i, [128, 8, 512], BF16) for i in range(3)]
    AB = T("AB", [128, 64])
    og = [T("og%d" % i, [128, TB]) for i in range(2)]

    lnA = T("lnA", [128, 32])
    P.op("dve", lambda e: e.tensor_scalar(out=lnA[:], in0=lnp[:, 0:32], scalar1=ALPHA, scalar2=None, op0=ALU.mult), reads=["lnp"], writes=["lnA"])
    P.op("dve", lambda e: e.tensor_tensor(out=AB[:, 0:16], in0=lnp[:, 0:16], in1=sc2p[:], op=ALU.mult), reads=["lnp", "sc2p"], writes=["AB"])
    P.op("dve", lambda e: e.tensor_tensor(out=AB[:, 16:32], in0=lnp[:, 16:32], in1=sc2p[:], op=ALU.mult), reads=["lnp", "sc2p", "AB"], writes=["AB"])
    P.op("dve", lambda e: e.tensor_tensor(out=AB[:, 16:32], in0=AB[:, 16:32], in1=mod[:, 48:64], op=ALU.add), reads=["AB", "mod"], writes=["AB"])

    nw16 = [0]
    nw2 = [0]
    nx = [0]
    pb_i = [0]

    def layer_norm_stats(tag):
        for k in range(16):
            zb, zq = zbs[k % 2], zqs[k % 2]
            zbk, zqk = "zb%d" % (k % 2), "zq%d" % (k % 2)
            P.op("act", lambda e, k=k, zb=zb: e.activation(out=zb[:], in_=z[:, k, :], func=AF.Copy), reads=["z%d" % k], writes=[zbk])
            P.op("pool", lambda e, k=k, zq=zq: e.tensor_tensor(out=zq[:], in0=z[:, k, :], in1=z[:, k, :], op=ALU.mult), reads=["z%d" % k], writes=[zqk])
            P.op("pe", lambda e, k=k, zb=zb: e.matmul(ps[4][:, :], lhsT=onesb[:, :], rhs=zb[:], start=(k == 0), stop=(k == 15)),
                 reads=[zbk, "onesb"], writes=["ps4"], sig=True)
            P.op("pe", lambda e, k=k, zq=zq: e.matmul(ps[5][:, :], lhsT=onesb[:, :], rhs=zq[:], start=(k == 0), stop=(k == 15)),
                 reads=[zqk, "onesb"], writes=["ps5"], sig=True)
        P.op("dve", lambda e: e.tensor_scalar(out=mean[:], in0=ps[4][:, :], scalar1=1.0 / D, scalar2=None, op0=ALU.mult), reads=["ps4"], writes=["mean"])
        P.op("dve", lambda e: e.tensor_tensor(out=tmp[:], in0=mean[:], in1=mean[:], op=ALU.mult), reads=["mean"], writes=["tmp"])
        P.op("dve", lambda e: e.scalar_tensor_tensor(out=rstd[:], in0=ps[5][:, :], scalar=1.0 / D, in1=tmp[:], op0=ALU.mult, op1=ALU.subtract),
             reads=["ps5", "tmp"], writes=["rstd"])
        P.op("dve", lambda e: e.tensor_scalar(out=rstd[:], in0=rstd[:], scalar1=EPS, scalar2=None, op0=ALU.add), reads=["rstd"], writes=["rstd"])
        P.op("act", lambda e: e.activation(out=rstd[:], in_=rstd[:], func=AF.Sqrt), reads=["rstd"], writes=["rstd"])
        P.op("dve", lambda e: e.reciprocal(out=rstd[:], in_=rstd[:]), reads=["rstd"], writes=["rstd"])

    for blk in range(4):
        tsl = slice(blk * TB, (blk + 1) * TB)
        xsl = slice(HALF + blk * TB, HALF + (blk + 1) * TB)
        P.dma("sp", yT[:], yT_d[:, tsl].rearrange("(k p) t -> p k t", p=128), writes=["yT"])
        for dg in range(4):
            w = w16[nw16[0] % 2]
            wk = "w16_%d" % (nw16[0] % 2)
            nw16[0] += 1
            P.dma("pool", w[:], wout_d[:, dg * 512:(dg + 1) * 512].rearrange("(k p) c -> p k c", p=128), writes=[wk])
            for dc in range(4):
                m = dg * 4 + dc
                pb = ps[pb_i[0] % 2]
                pk = "ps%d" % (pb_i[0] % 2)
                pb_i[0] += 1
                for k in range(16):
                    P.op("pe", lambda e, k=k, w=w, dc=dc, pb=pb: e.matmul(pb[:, :], lhsT=w[:, k, dc * 128:(dc + 1) * 128], rhs=yT[:, k, :], start=(k == 0), stop=(k == 15)),
                         reads=[wk, "yT"], writes=[pk], sig=(k == 15))
                x_ = xr[nx[0] % 2]
                xk = "xr%d" % (nx[0] % 2)
                nx[0] += 1
                P.dma("sp", x_[:], xT[m * 128:(m + 1) * 128, xsl], writes=[xk])
                P.op("act", lambda e, x_=x_: e.activation(out=x_[:], in_=x_[:], func=AF.Copy, scale=ALPHA), reads=[xk], writes=[xk])
                P.op("dve", lambda e, pb=pb, m=m, x_=x_: e.scalar_tensor_tensor(out=z[:, m, :], in0=pb[:, :], scalar=mod[:, 32 + m:33 + m], in1=x_[:],
                                                                             op0=ALU.mult, op1=ALU.add), reads=[pk, xk, "mod"], writes=["z%d" % m])
        layer_norm_stats("ln1")
        def ne(k):
            return "pool" if k % 3 == 2 else "dve"
        for k in range(16):
            P.op(ne(k), lambda e, k=k: e.tensor_tensor(out=z[:, k, :], in0=z[:, k, :], in1=mean[:], op=ALU.subtract), reads=["z%d" % k, "mean"], writes=["z%d" % k])
        for k in range(16):
            P.op(ne(k), lambda e, k=k: e.tensor_tensor(out=z[:, k, :], in0=z[:, k, :], in1=rstd[:], op=ALU.mult), reads=["z%d" % k, "rstd"], writes=["z%d" % k])
        for k in range(16):
            P.op("act", lambda e, k=k: e.activation(out=u2[:, k, :], in_=z[:, k, :], func=AF.Identity, scale=AB[:, k:k + 1], bias=AB[:, 16 + k:17 + k]),
                 reads=["z%d" % k, "AB"], writes=["yT"])
        for k in range(16):
            P.op("dve", lambda e, k=k: e.tensor_scalar(out=z[:, k, :], in0=z[:, k, :], scalar1=lnA[:, k:k + 1], scalar2=lnA[:, 16 + k:17 + k], op0=ALU.mult, op1=ALU.add),
                 reads=["z%d" % k, "lnA"], writes=["z%d" % k])
        for fg in range(16):
            w = w16[nw16[0] % 2]
            wk = "w16_%d" % (nw16[0] % 2)
            nw16[0] += 1
            P.dma("sp", w[:], wff1_d[:, fg * 512:(fg + 1) * 512].rearrange("(k p) c -> p k c", p=128), writes=[wk])
            for fc in range(4):
                f = fg * 4 + fc
                pb = ps[pb_i[0] % 2]
                pk = "ps%d" % (pb_i[0] % 2)
                pb_i[0] += 1
                for k in range(16):
                    P.op("pe", lambda e, k=k, w=w, fc=fc, pb=pb: e.matmul(pb[:, :], lhsT=w[:, k, fc * 128:(fc + 1) * 128], rhs=u2[:, k, :], start=(k == 0), stop=(k == 15)),
                         reads=[wk, "yT"], writes=[pk], sig=(k == 15))
                P.op("act", lambda e, pb=pb: e.activation(out=tmp[:], in_=pb[:, :], func=AF.Relu), reads=[pk], writes=["tmp"])
                P.op("dve", lambda e, f=f: e.tensor_tensor(out=hT[:, f, :], in0=tmp[:], in1=tmp[:], op=ALU.mult), reads=["tmp"], writes=["hT"])
        for dq in range(4):
            for f8 in range(8):
                w = w2[nw2[0] % 3]
                wk = "w2_%d" % (nw2[0] % 3)
                nw2[0] += 1
                P.dma("sp", w[:], wff2_d[f8 * 1024:(f8 + 1) * 1024, dq * 512:(dq + 1) * 512].rearrange("(f p) c -> p f c", p=128), writes=[wk])
                for dc in range(4):
                    for fl in range(8):
                        f = f8 * 8 + fl
                        P.op("pe", lambda e, w=w, fl=fl, dc=dc, f=f: e.matmul(ps[dc][:, :], lhsT=w[:, fl, dc * 128:(dc + 1) * 128], rhs=hT[:, f, :],
                                                                           start=(f == 0), stop=(f == 63)),
                             reads=[wk, "hT"], writes=["ps%d" % dc], sig=(fl == 7))
            for dc in range(4):
                m = dq * 4 + dc
                P.op("dve", lambda e, m=m, dc=dc: e.scalar_tensor_tensor(out=z[:, m, :], in0=ps[dc][:, :], scalar=mod[:, 80 + m:81 + m], in1=z[:, m, :],
                                                                        op0=ALU.mult, op1=ALU.add), reads=["ps%d" % dc, "mod", "z%d" % m], writes=["z%d" % m])
        layer_norm_stats("ln2")
        for k in range(16):
            P.op(ne(k), lambda e, k=k: e.tensor_tensor(out=z[:, k, :], in0=z[:, k, :], in1=mean[:], op=ALU.subtract), reads=["z%d" % k, "mean"], writes=["z%d" % k])
        for k in range(16):
            P.op(ne(k), lambda e, k=k: e.tensor_tensor(out=z[:, k, :], in0=z[:, k, :], in1=rstd[:], op=ALU.mult), reads=["z%d" % k, "rstd"], writes=["z%d" % k])
        for k in range(16):
            o_ = og[k % 2]
            okk = "og%d" % (k % 2)
            P.op("act", lambda e, k=k, o_=o_: e.activation(out=o_[:], in_=z[:, k, :], func=AF.Identity, scale=lnp[:, 32 + k:33 + k], bias=lnp[:, 48 + k:49 + k]),
                 reads=["z%d" % k, "lnp"], writes=[okk])
            P.dma("sp", outT[k * 128:(k + 1) * 128, tsl], o_[:], reads=[okk])


def _consts():
    c = {}
    c["ident_bf"] = np.eye(128, dtype=np.float32).astype(ml_dtypes.bfloat16)
    c["ident_f"] = np.eye(128, dtype=np.float32)
    s = np.arange(128)[:, None]
    t = np.arange(128)[None, :]
    same = (s // 64) == (t // 64)
    c["ucum"] = np.where(same & (s <= t), -1.0 / 16.0, 0.0).astype(np.float32)
    c["urev"] = np.where(same & (s > t), -1.0 / 16.0, 0.0).astype(np.float32)
    c["maskT"] = np.where(same & (s <= t), 1.0, 0.0).astype(np.float32)
    rm = np.ones((4, 512), np.float32)
    rm[:, ::64] = 0.0
    c["rmask"] = rm
    sel = np.zeros((4, 4, 128), np.float32)
    for h in range(4):
        sel[h, h, :] = 1.0
    c["sel"] = np.ascontiguousarray(sel.transpose(1, 0, 2).reshape(4, 512))
    c["ones_bf"] = np.ones((128, 128), np.float32).astype(ml_dtypes.bfloat16)
    return c


def _pk(v):
    return np.ascontiguousarray(np.asarray(v, np.float32).reshape(16, 128).T)


_NC = None


def kernel(x, c, w_ada, b_ada, w_in, gla_w_alpha_up, gla_b_alpha, gla_norm_g, mlstm_conv_w, mlstm_conv_b,
           mlstm_b_i, mlstm_b_f, mlstm_norm_g, w_out, ln1_g, ln1_b, w_ff1, w_ff2, ln2_g, ln2_b):
    global _NC
    f32 = np.float32
    x = np.asarray(x, f32)
    c = np.asarray(c, f32)
    w_in0 = np.asarray(w_in, f32)[0]
    sizes = [512, 512, 1024, 1024, 16, 512, 512, 1024, 1024, 4, 4]
    offs = np.cumsum([0] + sizes)
    gq, gk, gv, gr, ga, mq, mk, mv, mo, mi, mf = [w_in0[:, offs[i]:offs[i + 1]] for i in range(11)]
    w_tm = np.ascontiguousarray(np.concatenate([gv, gr, mv, mo], axis=1))
    small = np.zeros((D, 128), f32)
    small[:, 0:16] = ga
    small[:, 32:36] = mi
    small[:, 64:68] = mf
    w_fm = np.ascontiguousarray(np.concatenate([gq, gk, mq, mk, small], axis=1))
    wa = np.asarray(w_ada, f32)[0].reshape(D, 6, 16, 128).transpose(0, 1, 3, 2).reshape(D, 6 * D)
    wa = np.ascontiguousarray(wa)
    ba = np.ascontiguousarray(np.asarray(b_ada, f32)[0].reshape(6, 16, 128).transpose(2, 0, 1).reshape(128, 96))
    wup = np.zeros((32, 512), f32)
    wup[0:16] = np.asarray(gla_w_alpha_up, f32)[0]
    wup[16] = np.asarray(gla_b_alpha, f32)[0]
    ggla = np.ascontiguousarray(np.broadcast_to(np.asarray(gla_norm_g, f32)[0][None, :], (128, 1024)))
    gml = np.ascontiguousarray(np.broadcast_to(np.asarray(mlstm_norm_g, f32)[0][None, :], (128, 1024)))
    cw = np.asarray(mlstm_conv_w, f32)[0]
    convw = np.ascontiguousarray(cw.reshape(4, 8, 128).transpose(2, 1, 0).reshape(128, 32))
    convb = np.ascontiguousarray(np.asarray(mlstm_conv_b, f32)[0].reshape(8, 128).T)
    b_i = np.ascontiguousarray(np.asarray(mlstm_b_i, f32)[0].reshape(4, 1))
    b_f = np.ascontiguousarray(np.asarray(mlstm_b_f, f32)[0].reshape(4, 1))
    ln = np.ascontiguousarray(np.concatenate([_pk(ln1_g[0]), _pk(ln1_b[0]), _pk(ln2_g[0]), _pk(ln2_b[0])], axis=1))
    consts = _consts()
    shared = dict(w_ada=wa, b_ada=ba, w_tm=w_tm, w_fm=w_fm, w_up=wup, g_gla=ggla, g_ml=gml, convw=convw, convb=convb,
                  b_i=b_i, b_f=b_f, w_out=np.ascontiguousarray(np.asarray(w_out, f32)[0]),
                  w_ff1=np.ascontiguousarray(np.asarray(w_ff1, f32)[0]), w_ff2=np.ascontiguousarray(np.asarray(w_ff2, f32)[0]), ln=ln)
    shared.update(consts)
    in_maps = []
    for core in range(8):
        b, j = core // 2, core % 2
        xt = np.ascontiguousarray(x[b].T)
        if j == 0:
            xTc = np.concatenate([np.zeros((D, HALF), f32), xt[:, :HALF]], axis=1)
        else:
            xTc = xt
        m = dict(shared)
        m["xT"] = np.ascontiguousarray(xTc)
        m["flag"] = np.full((128, 1), float(j), f32)
        m["cT"] = _pk(c[b])
        in_maps.append(m)
    if _NC is None:
        _NC = build_nc()
    res = run_bass_kernel_spmd(_NC, in_maps, core_ids=list(range(8)), **({'trace': True} if TRACE else {}))
    out = np.empty((NB, SEQ, D), f32)
    for core in range(8):
        b, j = core // 2, core % 2
        out[b, j * HALF:(j + 1) * HALF, :] = np.asarray(res.results[core]["outT"]).T
    kernel.last_results = res
    return out
```
